# Optimizing a Trainium2 kernel written in Bass

```python
import math
import jax, jax.numpy as jnp
from jax import lax
import numpy as np

D_MODEL = 1024
BATCH = 16
SEQ = 4096
DEPTH = 4
DEC_BATCH = 4
DEC_SEQ = 8192
PAST_LEN = 128

N_MIXERS = 2
N_GLA_LAYERS = (DEPTH + N_MIXERS - 1) // N_MIXERS
N_MLA_LAYERS = DEPTH // N_MIXERS
D_FF = 2816
RES_HALF = 0.5
N_MOD = 9
EPS = 1e-6
GLA_HEADS = 4
GLA_DK = D_MODEL // 2 // GLA_HEADS
GLA_DV = D_MODEL // GLA_HEADS
GLA_GATE_RANK = 16
GLA_TAU = 16.0
GLA_CHUNK = 64
GLA_QK = GLA_HEADS * GLA_DK
GLA_VR = GLA_HEADS * GLA_DV
GLA_IN = 2 * GLA_QK + 2 * GLA_VR + 2 * GLA_GATE_RANK
MLA_HEADS = 8
MLA_NOPE = 128
MLA_ROPE = 64
MLA_V = 128
MLA_Q_RANK = 256
MLA_KV_RANK = 128
MLA_IN = MLA_Q_RANK + MLA_KV_RANK + MLA_ROPE
ROPE_THETA = 10000.0
Q_BLOCK = 128

kernel_name = 'hybrid_gla_mla_macaron_adaln_encoder'


def rms_norm(x, g):
    xf = x.astype(jnp.float32)
    y = xf * lax.rsqrt(jnp.mean(xf * xf, axis=-1, keepdims=True) + EPS)
    return (y * g.astype(jnp.float32)).astype(x.dtype)


def modulate(h, shift, scale):
    return h * (1 + scale[:, None, :]) + shift[:, None, :]


def swiglu_ffn(h, w_gate, w_up, w_down):
    return (jax.nn.silu(h @ w_gate) * (h @ w_up)) @ w_down


def gla_chunked(q, k, v, log_a):
    B, H, S, dk = q.shape
    dv = v.shape[-1]
    n = S // GLA_CHUNK

    def chunks(t):
        return t.reshape(B, H, n, GLA_CHUNK, t.shape[-1])

    q, k, v, log_a = chunks(q), chunks(k), chunks(v), chunks(log_a)
    b = lax.cumsum(log_a, axis=3)
    b_last = b[:, :, :, -1:, :]
    q_in = q * jnp.exp(b)
    k_in = k * jnp.exp(-b)
    k_out = k * jnp.exp(b_last - b)
    lower_tri = jnp.tril(jnp.ones((GLA_CHUNK, GLA_CHUNK), dtype=bool))
    a = jnp.einsum('bhnid,bhnjd->bhnij', q_in, k_in)
    a = jnp.where(lower_tri, a, 0.0)
    o_intra = jnp.einsum('bhnij,bhnjv->bhniv', a, v)

    def step(state, inp):
        q_c, k_c, v_c, decay = inp
        o_c = jnp.einsum('bhid,bhdv->bhiv', q_c, state)
        state = state * decay[..., None] + jnp.einsum('bhjd,bhjv->bhdv', k_c, v_c)
        return state, o_c

    decay = jnp.exp(b_last[:, :, :, 0, :])
    xs = (jnp.moveaxis(q_in, 2, 0), jnp.moveaxis(k_out, 2, 0),
          jnp.moveaxis(v, 2, 0), jnp.moveaxis(decay, 2, 0))
    state0 = jnp.zeros((B, H, dk, dv), jnp.float32)
    _, o_inter = lax.scan(step, state0, xs)
    o = o_intra + jnp.moveaxis(o_inter, 0, 2)
    return o.reshape(B, H, S, dv)


def gla_mixer(h, w_in, w_gate_up, b_gate, g_norm, w_out):
    B, S, _ = h.shape
    proj = h @ w_in
    cuts = [GLA_QK, 2 * GLA_QK, 2 * GLA_QK + GLA_VR, 2 * GLA_QK + 2 * GLA_VR,
            2 * GLA_QK + 2 * GLA_VR + GLA_GATE_RANK]
    q, k, v, r, a_fwd, a_bwd = jnp.split(proj, cuts, axis=-1)

    def heads(t, d):
        return t.reshape(B, S, GLA_HEADS, d).transpose(0, 2, 1, 3).astype(jnp.float32)

    q = heads(q, GLA_DK) * (GLA_DK ** -0.5)
    k = heads(k, GLA_DK)
    v = heads(v, GLA_DV)

    def log_gate(a_low, w_up, b):
        logits = (a_low @ w_up + b).astype(jnp.float32)
        return heads(jax.nn.log_sigmoid(logits) / GLA_TAU, GLA_DK)

    la_fwd = log_gate(a_fwd, w_gate_up[0], b_gate[0])
    la_bwd = log_gate(a_bwd, w_gate_up[1], b_gate[1])
    o_fwd = gla_chunked(q, k, v, la_fwd)
    flip = lambda t: jnp.flip(t, axis=2)
    o_bwd = flip(gla_chunked(flip(q), flip(k), flip(v), flip(la_bwd)))
    o = (o_fwd + o_bwd).transpose(0, 2, 1, 3)
    o = rms_norm(o, g_norm).reshape(B, S, GLA_VR).astype(h.dtype)
    return (jax.nn.silu(r) * o) @ w_out


def rope_tables(S):
    inv = 1.0 / (ROPE_THETA ** (jnp.arange(0, MLA_ROPE, 2, dtype=jnp.float32) / MLA_ROPE))
    ang = jnp.arange(S, dtype=jnp.float32)[:, None] * inv[None, :]
    return jnp.cos(ang)[:, None, :], jnp.sin(ang)[:, None, :]


def apply_rope(x, cos, sin):
    half = MLA_ROPE // 2
    x1, x2 = x[..., :half], x[..., half:]
    out = jnp.concatenate([x1 * cos - x2 * sin, x1 * sin + x2 * cos], axis=-1)
    return out.astype(x.dtype)


def mla_mixer(h, w_in, g_q, g_kv, w_uq, w_ukv, w_out):
    B, S, _ = h.shape
    c_q, c_kv, k_rope = jnp.split(h @ w_in, [MLA_Q_RANK, MLA_Q_RANK + MLA_KV_RANK], axis=-1)
    c_q = rms_norm(c_q, g_q)
    c_kv = rms_norm(c_kv, g_kv)
    q = (c_q @ w_uq).reshape(B, S, MLA_HEADS, MLA_NOPE + MLA_ROPE)
    q_nope, q_rope = q[..., :MLA_NOPE], q[..., MLA_NOPE:]
    kv = (c_kv @ w_ukv).reshape(B, S, MLA_HEADS, MLA_NOPE + MLA_V)
    k_nope, v = kv[..., :MLA_NOPE], kv[..., MLA_NOPE:]
    cos, sin = rope_tables(S)
    q_rope = apply_rope(q_rope, cos, sin)
    k_rope = apply_rope(k_rope[:, :, None, :], cos, sin)[:, :, 0, :]
    scale = (MLA_NOPE + MLA_ROPE) ** -0.5
    nb = S // Q_BLOCK

    def blocks(t):
        return jnp.moveaxis(t.reshape(B, nb, Q_BLOCK, MLA_HEADS, t.shape[-1]), 1, 0)

    def attend(qs):
        qn, qr = qs
        s = (jnp.einsum('bqhd,bkhd->bhqk', qn, k_nope, preferred_element_type=jnp.float32)
             + jnp.einsum('bqhr,bkr->bhqk', qr, k_rope, preferred_element_type=jnp.float32)) * scale
        p = jax.nn.softmax(s, axis=-1).astype(v.dtype)
        return jnp.einsum('bhqk,bkhd->bqhd', p, v)

    o = lax.map(attend, (blocks(q_nope), blocks(q_rope)))
    o = jnp.moveaxis(o, 0, 1).reshape(B, S, MLA_HEADS * MLA_V)
    return o @ w_out


def trunk(x, c, ada_w, ada_b, norm_g, ffn_w_gate, ffn_w_up, ffn_w_down,
          gla_w_in, gla_w_gate_up, gla_b_gate, gla_g_norm, gla_w_out,
          mla_w_in, mla_g_q, mla_g_kv, mla_w_uq, mla_w_ukv, mla_w_out,
          final_ada_w, final_ada_b, final_g):
    c_act = jax.nn.silu(c)
    for i in range(DEPTH):
        mod = c_act @ ada_w[i] + ada_b[i]
        s1, sc1, g1, sm, scm, gm, s2, sc2, g2 = jnp.split(mod, N_MOD, axis=-1)
        h = modulate(rms_norm(x, norm_g[i, 0]), s1, sc1)
        x = x + RES_HALF * g1[:, None, :] * swiglu_ffn(h, ffn_w_gate[i, 0], ffn_w_up[i, 0], ffn_w_down[i, 0])
        h = modulate(rms_norm(x, norm_g[i, 1]), sm, scm)
        j = i // N_MIXERS
        if i % N_MIXERS == 0:
            y = gla_mixer(h, gla_w_in[j], gla_w_gate_up[j], gla_b_gate[j], gla_g_norm[j], gla_w_out[j])
        else:
            y = mla_mixer(h, mla_w_in[j], mla_g_q[j], mla_g_kv[j], mla_w_uq[j], mla_w_ukv[j], mla_w_out[j])
        x = x + gm[:, None, :] * y
        h = modulate(rms_norm(x, norm_g[i, 2]), s2, sc2)
        x = x + RES_HALF * g2[:, None, :] * swiglu_ffn(h, ffn_w_gate[i, 1], ffn_w_up[i, 1], ffn_w_down[i, 1])
    fin_shift, fin_scale = jnp.split(c_act @ final_ada_w + final_ada_b, 2, axis=-1)
    return modulate(rms_norm(x, final_g), fin_shift, fin_scale)


def setup_inputs(seed: int = 0) -> dict:
    key = jax.random.key(seed)
    ks = jax.random.split(key, 24)
    f32 = jnp.float32

    def w(k, shape, fan_in):
        return jax.random.normal(k, shape, f32) * (fan_in ** -0.5)

    def gain(k, shape):
        return 1.0 + 0.02 * jax.random.normal(k, shape, f32)

    def bias(k, shape):
        return 0.02 * jax.random.normal(k, shape, f32)

    D = D_MODEL
    return {
        'x_prompt': jax.random.normal(ks[0], (BATCH, SEQ, D), f32),
        'x_sample': jax.random.normal(ks[1], (DEC_BATCH, DEC_SEQ, D), f32),
        'c_prompt': jax.random.normal(ks[2], (BATCH, D), f32),
        'c_sample': jax.random.normal(ks[3], (DEC_BATCH, D), f32),
        'ada_w': w(ks[4], (DEPTH, D, N_MOD * D), D),
        'ada_b': bias(ks[5], (DEPTH, N_MOD * D)),
        'norm_g': gain(ks[6], (DEPTH, 3, D)),
        'ffn_w_gate': w(ks[7], (DEPTH, 2, D, D_FF), D),
        'ffn_w_up': w(ks[8], (DEPTH, 2, D, D_FF), D),
        'ffn_w_down': w(ks[9], (DEPTH, 2, D_FF, D), D_FF),
        'gla_w_in': w(ks[10], (N_GLA_LAYERS, D, GLA_IN), D),
        'gla_w_gate_up': w(ks[11], (N_GLA_LAYERS, 2, GLA_GATE_RANK, GLA_QK), GLA_GATE_RANK),
        'gla_b_gate': bias(ks[12], (N_GLA_LAYERS, 2, GLA_QK)),
        'gla_g_norm': gain(ks[13], (N_GLA_LAYERS, GLA_DV)),
        'gla_w_out': w(ks[14], (N_GLA_LAYERS, GLA_VR, D), GLA_VR),
        'mla_w_in': w(ks[15], (N_MLA_LAYERS, D, MLA_IN), D),
        'mla_g_q': gain(ks[16], (N_MLA_LAYERS, MLA_Q_RANK)),
        'mla_g_kv': gain(ks[17], (N_MLA_LAYERS, MLA_KV_RANK)),
        'mla_w_uq': w(ks[18], (N_MLA_LAYERS, MLA_Q_RANK, MLA_HEADS * (MLA_NOPE + MLA_ROPE)), MLA_Q_RANK),
        'mla_w_ukv': w(ks[19], (N_MLA_LAYERS, MLA_KV_RANK, MLA_HEADS * (MLA_NOPE + MLA_V)), MLA_KV_RANK),
        'mla_w_out': w(ks[20], (N_MLA_LAYERS, MLA_HEADS * MLA_V, D), MLA_HEADS * MLA_V),
        'final_ada_w': w(ks[21], (D, 2 * D), D),
        'final_ada_b': bias(ks[22], (2 * D,)),
        'final_g': gain(ks[23], (D,)),
    }


def reference(x_prompt, x_sample, c_prompt, c_sample, ada_w, ada_b, norm_g,
              ffn_w_gate, ffn_w_up, ffn_w_down,
              gla_w_in, gla_w_gate_up, gla_b_gate, gla_g_norm, gla_w_out,
              mla_w_in, mla_g_q, mla_g_kv, mla_w_uq, mla_w_ukv, mla_w_out,
              final_ada_w, final_ada_b, final_g):
    y_prompt = trunk(x_prompt, c_prompt, ada_w, ada_b, norm_g, ffn_w_gate, ffn_w_up, ffn_w_down,
                     gla_w_in, gla_w_gate_up, gla_b_gate, gla_g_norm, gla_w_out,
                     mla_w_in, mla_g_q, mla_g_kv, mla_w_uq, mla_w_ukv, mla_w_out,
                     final_ada_w, final_ada_b, final_g)
    y_sample = trunk(x_sample, c_sample, ada_w, ada_b, norm_g, ffn_w_gate, ffn_w_up, ffn_w_down,
                     gla_w_in, gla_w_gate_up, gla_b_gate, gla_g_norm, gla_w_out,
                     mla_w_in, mla_g_q, mla_g_kv, mla_w_uq, mla_w_ukv, mla_w_out,
                     final_ada_w, final_ada_b, final_g)
    return (y_prompt, y_sample)
```

```python
import numpy as np
import concourse.bass as bass
import concourse.mybir as mybir
from concourse.bass_utils import run_bass_kernel_spmd

F32 = mybir.dt.float32
BF16 = mybir.dt.bfloat16
AF = mybir.ActivationFunctionType
ALU = mybir.AluOpType
AX = mybir.AxisListType

SEM_LIMIT = 30000
DMA_SLOTS = 6
SAME_ENGINE_SYNC = True


class Prog:
    def __init__(self, nc):
        self.nc = nc
        self.ops = []

    def op(self, eng, fn, reads=(), writes=(), dma=False):
        self.ops.append((eng, fn, tuple(reads), tuple(writes), dma))

    def barrier(self):
        self.ops.append(('barrier', None, (), (), False))

    def dma(self, q, out, in_, reads=(), writes=(), **kw):
        self.op(q, lambda e: e.dma_start(out=out, in_=in_, **kw), reads, writes, dma=True)

    def emit(self):
        nc = self.nc
        ops = self.ops
        n = len(ops)
        last_w = {}
        readers = {}
        deps = [None] * n
        last_eng = {}
        recent_dma = {}
        pend = {}
        for i, (eng, fn, rd, wr, isdma) in enumerate(ops):
            if eng == 'barrier':
                snap = set(last_eng.values())
                for q, l in recent_dma.items():
                    snap.update(l[-DMA_SLOTS:])
                for en in ('pe', 'act', 'dve', 'pool', 'sp'):
                    pend.setdefault(en, set()).update(snap)
                last_w = {}
                readers = {}
                deps[i] = []
                continue
            d = set()
            if eng in pend:
                d.update(pend.pop(eng))
            last_eng[eng] = i
            if isdma:
                recent_dma.setdefault(eng, []).append(i)
            for r in rd:
                j = last_w.get(r)
                if j is not None:
                    d.add(j)
            for w in wr:
                j = last_w.get(w)
                if j is not None:
                    d.add(j)
                d.update(readers.get(w, {}).values())
            d.discard(i)
            for r in rd:
                rl = readers.setdefault(r, {})
                rl[i if isdma else eng] = i
            for w in wr:
                last_w[w] = i
                readers[w] = {}
            dd = []
            for j in d:
                je, _, _, _, jd = ops[j]
                if not jd and je == eng:
                    if eng == 'pe' or not SAME_ENGINE_SYNC:
                        continue
                dd.append(j)
            deps[i] = sorted(dd)
        needed = set()
        for d in deps:
            needed.update(d)
        sig = {}
        cnt = {}
        dma_hist = {}
        sems = {}

        def getsem(key):
            if key not in sems:
                sems[key] = nc.alloc_semaphore("s_%s" % "_".join(str(x) for x in key))
            return sems[key]

        extra_dep = {}
        for i, (eng, fn, rd, wr, isdma) in enumerate(ops):
            if eng == 'barrier':
                continue
            if isdma:
                h = dma_hist.setdefault(eng, [])
                k = len(h)
                slot = k % DMA_SLOTS
                r = k // DMA_SLOTS
                ep = r // (SEM_LIMIT // 16)
                val = 16 * (r % (SEM_LIMIT // 16) + 1)
                sig[i] = (('d', eng, slot, ep), val, k)
                if k >= DMA_SLOTS:
                    extra_dep[i] = h[k - DMA_SLOTS]
                h.append(i)
            elif i in needed:
                k = cnt.get(eng, 0)
                cnt[eng] = k + 1
                sig[i] = (('c', eng, k // SEM_LIMIT), k % SEM_LIMIT + 1, k)
        self.nsig = dict(cnt)
        per_eng = {}
        for i, o in enumerate(ops):
            if o[0] != 'barrier':
                per_eng.setdefault(o[0], []).append(i)

        def emit_engine(ename, e):
            waited_k = {}
            waited_s = {}
            nwait = 0
            for i in per_eng.get(ename, ()):
                eng, fn, rd, wr, isdma = ops[i]
                dl = list(deps[i])
                if i in extra_dep:
                    dl.append(extra_dep[i])
                for j in dl:
                    key, val, k = sig[j]
                    if key[0] == 'c':
                        if waited_k.get(key[1], -1) >= k:
                            continue
                        waited_k[key[1]] = k
                    else:
                        if waited_s.get(key, 0) >= val:
                            continue
                        waited_s[key] = val
                    e.wait_ge(getsem(key), val)
                    nwait += 1
                ins = fn(e)
                if i in sig:
                    key, val, k = sig[i]
                    ins.then_inc(getsem(key), 16 if isdma else 1)
            h = dma_hist.get(ename, [])
            for j in h[-DMA_SLOTS:]:
                key, val, k = sig[j]
                if waited_s.get(key, 0) >= val:
                    continue
                waited_s[key] = val
                e.wait_ge(getsem(key), val)
            self.nwait = getattr(self, 'nwait', 0) + nwait

        with nc.Block() as block:
            @block.tensor
            def _(e):
                emit_engine('pe', e)

            @block.scalar
            def _(e):
                emit_engine('act', e)

            @block.vector
            def _(e):
                emit_engine('dve', e)

            @block.gpsimd
            def _(e):
                emit_engine('pool', e)

            @block.sync
            def _(e):
                emit_engine('sp', e)


D = 1024
DFF = 2816
NCH = 8
FCH = 22
T = 512
EPS = 1e-6
NMOD = 9


def prod(l):
    r = 1
    for v in l:
        r *= int(v)
    return r


class Arena:
    def __init__(self, t, nbytes):
        self.t = t
        self.nbytes = nbytes
        self.off = 0

    def alloc(self, free_shape, dt, parts=128):
        n = prod(free_shape)
        sz = 4 if dt == F32 else 2
        nb = (n * sz + 31) // 32 * 32
        assert self.off + nb <= self.nbytes, ("arena overflow", self.off, nb, self.nbytes)
        w0 = self.off // 4
        ap = self.t[0:parts, w0:w0 + (n * sz + 3) // 4]
        if dt != F32:
            ap = ap.bitcast(dt)
        if len(free_shape) == 2:
            ap = ap.rearrange("p (a b) -> p a b", b=int(free_shape[1]))
        elif len(free_shape) == 3:
            ap = ap.rearrange("p (a b c) -> p a b c", b=int(free_shape[1]), c=int(free_shape[2]))
        self.off += nb
        return ap


class K:
    def __init__(self, cfg):
        self.cfg = cfg
        self.SEG = cfg['SEG']
        self.NT = 3 * self.SEG
        self.NTILE = self.NT // T
        self.depth = cfg.get('depth', 4)
        self.mixers = cfg.get('mixers', True)
        nc = self.nc = bass.Bass("TRN2", target_bir_lowering=False)
        self.p = Prog(nc)
        NT = self.NT

        def din(name, shape, dt=F32):
            return nc.dram_tensor(name, list(shape), dt, kind="ExternalInput").ap()

        def dscr(name, shape, dt=F32):
            return nc.dram_tensor(name, list(shape), dt, kind="Internal").ap()

        self.x_in = din("x", [NT, D])
        self.c_in = din("crows", [24, 128])
        self.flags_in = din("flags", [128, 4])
        self.pos_in = din("pos", [1, NT])
        self.ident_in = din("ident", [128, 128])
        self.tri_in = din("tri", [4, 128, 128])
        self.invf_in = din("invf", [64, 1])
        W = self.W = {}
        W['ada_w'] = din("ada_w", [4, D, NMOD * D])
        W['ada_b'] = din("ada_b", [4, NMOD * D])
        W['norm_g'] = din("norm_g", [4, 3, D])
        W['ffn_w_gate'] = din("ffn_w_gate", [4, 2, D, DFF])
        W['ffn_w_up'] = din("ffn_w_up", [4, 2, D, DFF])
        W['ffn_w_down'] = din("ffn_w_down", [4, 2, DFF, D])
        W['gla_w_in'] = din("gla_w_in", [2, D, 3104])
        W['gla_w_gate_up'] = din("gla_w_gate_up", [2, 2, 16, 512])
        W['gla_b_gate'] = din("gla_b_gate", [2, 2, 512])
        W['gla_g_norm'] = din("gla_g_norm", [2, 256])
        W['gla_w_out'] = din("gla_w_out", [2, D, D])
        W['mla_w_in'] = din("mla_w_in", [2, D, 448])
        W['mla_g_q'] = din("mla_g_q", [2, 256])
        W['mla_g_kv'] = din("mla_g_kv", [2, 128])
        W['mla_w_uq'] = din("mla_w_uq", [2, 256, 1536])
        W['mla_w_ukv'] = din("mla_w_ukv", [2, 128, 2048])
        W['mla_w_out'] = din("mla_w_out", [2, D, D])
        W['final_ada_w'] = din("final_ada_w", [D, 2 * D])
        W['final_ada_b'] = din("final_ada_b", [2 * D])
        W['final_g'] = din("final_g", [D])
        self.y_out = nc.dram_tensor("y", [NT, D], F32, kind="ExternalOutput").ap()
        self.xT = dscr("xT", [NCH, 128, NT])
        self.dscr = dscr

        nbytes = (nc.sbuf_bytes_remaining - 256) // 32 * 32
        self.arena_t = nc.alloc_sbuf_tensor("arena", [128, nbytes // 4], F32)
        self.ar = Arena(self.arena_t, nbytes)
        self.pp = [nc.alloc_psum_tensor("pp%d" % i, [128, 1024], F32)[:] for i in range(4)]
        self.ps = [self.pp[i // 2][:, (i % 2) * 512:(i % 2 + 1) * 512] for i in range(8)]

    def MM(self, out, lhsT, rhs, start, stop, rd, wr):
        self.p.op('pe', lambda e: e.matmul(out, lhsT, rhs, start=start, stop=stop), rd, wr)

    def TR(self, out, in_, ident, rd, wr):
        self.p.op('pe', lambda e: e.transpose(out, in_, ident), rd, wr)

    def ACT(self, out, in_, func, rd, wr, bias=0.0, scale=1.0):
        self.p.op('act', lambda e: e.activation(out=out, in_=in_, func=func, bias=bias, scale=scale), rd, wr)

    def TS(self, eng, out, in0, s1, s2, op0, op1, rd, wr):
        if s2 is None:
            self.p.op(eng, lambda e: e.tensor_scalar(out=out, in0=in0, scalar1=s1, scalar2=None, op0=op0), rd, wr)
        else:
            self.p.op(eng, lambda e: e.tensor_scalar(out=out, in0=in0, scalar1=s1, scalar2=s2, op0=op0, op1=op1), rd, wr)

    def STT(self, eng, out, in0, scalar, in1, op0, op1, rd, wr):
        self.p.op(eng, lambda e: e.scalar_tensor_tensor(out=out, in0=in0, scalar=scalar, in1=in1, op0=op0, op1=op1), rd, wr)

    def TT(self, eng, out, in0, in1, op, rd, wr):
        self.p.op(eng, lambda e: e.tensor_tensor(out=out, in0=in0, in1=in1, op=op), rd, wr)

    def CP(self, eng, out, in_, rd, wr):
        if eng == 'act':
            self.p.op('act', lambda e: e.copy(out=out, in_=in_), rd, wr)
        else:
            self.p.op(eng, lambda e: e.tensor_copy(out=out, in_=in_), rd, wr)

    def MEMSET(self, eng, ap, val, wr):
        self.p.op(eng, lambda e: e.memset(ap, val), (), wr)

    def DMA(self, q, out, in_, rd, wr):
        self.p.dma(q, out, in_, rd, wr)

    def load_cols(self, rows_ap, R, dest, dest_res, tag):
        st = self.stage_rows
        self.DMA('sp', st[0:R, :], rows_ap, [], ['stage_rows'])
        self.TR(self.ps[7][:, 0:R], st[0:R, :], self.ident[0:R, 0:R], ['stage_rows', 'ident'], ['ps7'])
        self.CP('dve', dest, self.ps[7][:, 0:R], ['ps7'], [dest_res])

    def prologue(self):
        ar = self.ar
        W = self.W
        depth = self.depth
        self.ident = ar.alloc([128], F32)
        self.ones_bf = ar.alloc([128], BF16)
        self.flags = ar.alloc([4], F32)
        self.modT = ar.alloc([4, 216], F32)
        self.Aar = ar.alloc([4, 3, 24], F32)
        self.Gar = ar.alloc([4, 3, 24], F32)
        self.finmod = ar.alloc([48], F32)
        self.Afin = ar.alloc([24], F32)
        self.normg = ar.alloc([96], F32)
        self.fing = ar.alloc([8], F32)
        self.cact = ar.alloc([24], BF16)
        self.persist_mark = ar.off
        self.stage_rows = ar.alloc([128], F32)
        cT = ar.alloc([24], F32)
        adab = ar.alloc([72], F32)
        finb = ar.alloc([16], F32)
        wblk = [ar.alloc([8, 1024], BF16) for _ in range(2)]
        self.DMA('sp', self.ident, self.ident_in, [], ['ident'])
        self.DMA('sp', self.flags, self.flags_in, [], ['flags'])
        self.MEMSET('dve', self.ones_bf, 1.0, ['ones'])
        self.load_cols(self.c_in, 24, cT, 'cT', 'c')
        self.ACT(self.cact, cT, AF.Silu, ['cT'], ['cact'])
        self.load_cols(W['norm_g'].rearrange("l j (c p) -> (l j c) p", p=128), 96, self.normg, 'normg', 'ng')
        self.load_cols(W['final_g'].rearrange("(c p) -> c p", p=128), 8, self.fing, 'fing', 'fg')
        self.load_cols(W['final_ada_b'].rearrange("(c p) -> c p", p=128), 16, finb, 'finb', 'fb')
        nblk = 0

        def mod_block(wsrc, bias_t, bias_res, mc0, dest, dest_res):
            nonlocal nblk
            wb = wblk[nblk % 2]
            wres = 'wblk%d' % (nblk % 2)
            nblk += 1
            self.DMA('pool', wb, wsrc, [], [wres])
            pst = self.ps[nblk % 2]
            psr = 'ps%d' % (nblk % 2)
            for mcl in range(8):
                for k in range(8):
                    self.MM(pst[:, mcl * 3:(mcl + 1) * 3], wb[:, k, mcl * 128:(mcl + 1) * 128],
                            self.cact[:, k * 3:(k + 1) * 3], k == 0, k == 7, [wres, 'cact'], [psr])
            for mcl in range(8):
                mc = mc0 + mcl
                self.TS('dve', dest[:, mc * 3:(mc + 1) * 3], pst[:, mcl * 3:(mcl + 1) * 3],
                        bias_t[:, mc:mc + 1], None, ALU.add, None, [psr, bias_res], [dest_res])

        for L in range(depth):
            self.load_cols(W['ada_b'][L].rearrange("(c p) -> c p", p=128), 72, adab, 'adab', 'ab')
            wv = W['ada_w'][L].rearrange("(k p) n -> p k n", p=128)
            for blk in range(9):
                mod_block(wv[:, :, blk * 1024:(blk + 1) * 1024], adab, 'adab', blk * 8, self.modT[:, L, :], 'modT')
            for j in range(3):
                for c in range(8):
                    col = ((3 * j + 1) * 8 + c) * 3
                    self.TS('dve', self.Aar[:, L, j, c * 3:(c + 1) * 3], self.modT[:, L, col:col + 3], 1.0,
                            self.normg[:, (L * 3 + j) * 8 + c:(L * 3 + j) * 8 + c + 1], ALU.add, ALU.mult,
                            ['modT', 'normg'], ['Aar'])
                col = ((3 * j + 2) * 8) * 3
                self.TS('dve', self.Gar[:, L, j, :], self.modT[:, L, col:col + 24], 1.0 if j == 1 else 0.5, None,
                        ALU.mult, None, ['modT'], ['Gar'])
        wv = W['final_ada_w'].rearrange("(k p) n -> p k n", p=128)
        for blk in range(2):
            mod_block(wv[:, :, blk * 1024:(blk + 1) * 1024], finb, 'finb', blk * 8, self.finmod, 'finmod')
        for c in range(8):
            self.TS('dve', self.Afin[:, c * 3:(c + 1) * 3], self.finmod[:, (8 + c) * 3:(8 + c) * 3 + 3], 1.0,
                    self.fing[:, c:c + 1], ALU.add, ALU.mult, ['finmod', 'fing'], ['Afin'])

    def Acol(self, L, j, c, s):
        return self.Aar[:, L, j, c * 3 + s:c * 3 + s + 1]

    def Bcol(self, L, j, c, s):
        col = ((3 * j) * 8 + c) * 3 + s
        return self.modT[:, L, col:col + 1]

    def Gcol(self, L, j, c, s):
        return self.Gar[:, L, j, c * 3 + s:c * 3 + s + 1]

    def new_phase(self):
        self.p.barrier()
        self.ar.off = self.persist_mark

    def norm_mod(self, xt, xres, sq, sqres, ssq_ps, ssq_res, rstd, tmp, hout, hres, Acol, Bcol):
        for c in range(8):
            self.ACT(sq[:, c, :], xt[:, c, :], AF.Square, [xres(c)], [sqres(c)])
        for c in range(8):
            self.MM(ssq_ps, self.ones_bf, sq[:, c, :], c == 0, c == 7, ['ones', sqres(c)], [ssq_res])
        self.ACT(rstd, ssq_ps, AF.Sqrt, [ssq_res, 'epsc'], ['rstd'], bias=self.epsc, scale=1.0 / D)
        self.p.op('dve', lambda e: e.reciprocal(out=rstd, in_=rstd), ['rstd'], ['rstd'])
        for c in range(8):
            tm = tmp[c % len(tmp)]
            tr = 'tmp%d' % (c % len(tmp))
            self.STT('dve', tm, xt[:, c, :], Acol(c), rstd, ALU.mult, ALU.mult, [xres(c), 'rstd', 'Aar'], [tr])
            self.ACT(hout[:, c, :], tm, AF.Identity, [tr, 'modT'], [hres(c)], bias=Bcol(c), scale=1.0)

    def pass_T(self):
        self.new_phase()
        ar = self.ar
        xin = [ar.alloc([D], F32) for _ in range(4)]
        xs = [ar.alloc([8, T], F32) for _ in range(2)]
        xTv = self.xT.rearrange("c p t -> p c t")
        n = 0
        for t in range(self.NTILE):
            t0 = t * T
            for i in range(4):
                self.DMA('sp', xin[i], self.x_in[t0 + i * 128:t0 + (i + 1) * 128, :], [], ['xin%d' % i])
            xo = xs[t % 2]
            xr = 'xs%d' % (t % 2)
            for c in range(8):
                b = n % 2
                n += 1
                for i in range(4):
                    self.TR(self.ps[b][:, i * 128:(i + 1) * 128], xin[i][:, c * 128:(c + 1) * 128], self.ident,
                            ['xin%d' % i, 'ident'], ['ps%d' % b])
                self.CP('dve' if c % 2 == 0 else 'act', xo[:, c, :], self.ps[b], ['ps%d' % b], [(xr, c)])
            self.DMA('sp', xTv[:, :, t0:t0 + T], xo, [(xr, c) for c in range(8)], [])

    def pass_O(self):
        self.new_phase()
        ar = self.ar
        self.epsc = ar.alloc([1], F32)
        self.MEMSET('dve', self.epsc, EPS, ['epsc'])
        xt = [ar.alloc([8, T], F32) for _ in range(2)]
        sq = ar.alloc([8, T], BF16)
        rstd = ar.alloc([T], F32)
        tmp = [ar.alloc([T], F32) for _ in range(2)]
        hf = ar.alloc([8, T], F32)
        yo = [ar.alloc([D], F32) for _ in range(2)]
        xTv = self.xT.rearrange("c p t -> p c t")
        n = 0
        ny = 0
        for t in range(self.NTILE):
            t0 = t * T
            s = t0 // self.SEG
            x = xt[t % 2]
            xr = 'xt%d' % (t % 2)
            self.DMA('sp', x, xTv[:, :, t0:t0 + T], [], [(xr, c) for c in range(8)])
            self.norm_mod(x, lambda c: (xr, c), sq, lambda c: ('sq', c), self.ps[6], 'ps6', rstd, tmp, hf,
                          lambda c: ('hf', c),
                          lambda c: self.Afin[:, c * 3 + s:c * 3 + s + 1],
                          lambda c: self.finmod[:, c * 3 + s:c * 3 + s + 1])
            for i in range(4):
                y = yo[ny % 2]
                yr = 'yo%d' % (ny % 2)
                ny += 1
                for half in range(2):
                    b = n % 2
                    n += 1
                    for c4 in range(4):
                        c = half * 4 + c4
                        self.TR(self.ps[b][:, c4 * 128:(c4 + 1) * 128], hf[:, c, i * 128:(i + 1) * 128], self.ident,
                                [('hf', c), 'ident'], ['ps%d' % b])
                    self.CP('dve' if half == 0 else 'act', y[:, half * 512:(half + 1) * 512], self.ps[b],
                            ['ps%d' % b], [(yr, half)])
                self.DMA('sp', self.y_out[t0 + i * 128:t0 + (i + 1) * 128, :], y, [(yr, 0), (yr, 1)], [])

    def pass_F(self, L, f):
        self.new_phase()
        ar = self.ar
        W = self.W
        j = 0 if f == 0 else 2
        self.epsc = ar.alloc([1], F32)
        self.MEMSET('dve', self.epsc, EPS, ['epsc'])
        wg = ar.alloc([8, DFF], BF16)
        wu = ar.alloc([8, DFF], BF16)
        wd = ar.alloc([FCH, D], BF16)
        xt = [ar.alloc([8, T], F32) for _ in range(2)]
        h = ar.alloc([8, T], BF16)
        act = ar.alloc([FCH, T], BF16)
        rstd = ar.alloc([T], F32)
        tmp = [ar.alloc([T], F32) for _ in range(1)]
        sg = [ar.alloc([T], BF16) for _ in range(2)]
        wgv = W['ffn_w_gate'][L, f].rearrange("(k p) n -> p k n", p=128)
        wuv = W['ffn_w_up'][L, f].rearrange("(k p) n -> p k n", p=128)
        wdv = W['ffn_w_down'][L, f].rearrange("(k p) n -> p k n", p=128)
        for k in range(8):
            self.DMA('pool', wg[:, k, :], wgv[:, k, :], [], [('wg', k)])
            self.DMA('pool', wu[:, k, :], wuv[:, k, :], [], [('wu', k)])
        for k in range(0, FCH, 2):
            self.DMA('pool', wd[:, k:k + 2, :], wdv[:, k:k + 2, :], [], [('wd', k), ('wd', k + 1)])
        xTv = self.xT.rearrange("c p t -> p c t")

        def load(t):
            self.DMA('sp', xt[t % 2], xTv[:, :, t * T:(t + 1) * T], [], [('xt%d' % (t % 2), c) for c in range(8)])

        load(0)
        ng = 0
        nd = 0
        for t in range(self.NTILE):
            t0 = t * T
            s = t0 // self.SEG
            if t + 1 < self.NTILE:
                load(t + 1)
            x = xt[t % 2]
            xr = 'xt%d' % (t % 2)
            self.norm_mod(x, lambda c: (xr, c), act, lambda c: ('act', c), self.ps[6], 'ps6', rstd, tmp, h,
                          lambda c: ('h', c),
                          lambda c: self.Acol(L, j, c, s), lambda c: self.Bcol(L, j, c, s))
            for m in range(FCH):
                b = ng % 2
                ng += 1
                pg, pgr = self.ps[b], 'ps%d' % b
                pu, pur = self.ps[2 + b], 'ps%d' % (2 + b)
                for k in range(8):
                    self.MM(pg, wg[:, k, m * 128:(m + 1) * 128], h[:, k, :], k == 0, k == 7, [('wg', k), ('h', k)], [pgr])
                for k in range(8):
                    self.MM(pu, wu[:, k, m * 128:(m + 1) * 128], h[:, k, :], k == 0, k == 7, [('wu', k), ('h', k)], [pur])
                self.ACT(sg[b], pg, AF.Silu, [pgr], ['sg%d' % b])
                self.TT('dve', act[:, m, :], sg[b], pu, ALU.mult, ['sg%d' % b, pur], [('act', m)])
            for mo in range(8):
                b = nd % 2
                nd += 1
                pd, pdr = self.ps[4 + b], 'ps%d' % (4 + b)
                for k in range(FCH):
                    self.MM(pd, wd[:, k, mo * 128:(mo + 1) * 128], act[:, k, :], k == 0, k == FCH - 1,
                            [('wd', k), ('act', k)], [pdr])
                self.STT('dve', x[:, mo, :], pd, self.Gcol(L, j, mo, s), x[:, mo, :], ALU.mult, ALU.add,
                         [pdr, 'Gar', (xr, mo)], [(xr, mo)])
            self.DMA('sp', xTv[:, :, t0:t0 + T], x, [(xr, c) for c in range(8)], [])


    def rope_tables(self):
        self.new_phase()
        ar = self.ar
        NT = self.NT
        self.cosT = self.dscr("cosT", [64, NT])
        self.sinT = self.dscr("sinT", [64, NT])
        invf = ar.alloc([1], F32)
        self.DMA('sp', invf[0:64, :], self.invf_in, [], ['invf'])
        MAGIC = 12582912.0
        C1 = 6.28125
        C2 = float(2 * np.pi - 6.28125)
        PI = float(np.pi)
        bufs = [[ar.alloc([T], F32) for _ in range(6)] for _ in range(2)]
        for t in range(self.NTILE):
            t0 = t * T
            pos, ang, kf, r, rc, o = [b[0:64, :] for b in bufs[t % 2]]
            R = lambda nm: '%s%d' % (nm, t % 2)
            self.DMA('sp', pos, self.pos_in[0:1, t0:t0 + T].partition_broadcast(64), [], [R('pos')])
            self.TS('dve', ang, pos, invf[0:64, 0:1], None, ALU.mult, None, [R('pos'), 'invf'], [R('ang')])
            self.TS('dve', kf, ang, float(1 / (2 * np.pi)), MAGIC, ALU.mult, ALU.add, [R('ang')], [R('kf')])
            self.TS('dve', kf, kf, MAGIC, None, ALU.subtract, None, [R('kf')], [R('kf')])
            self.STT('dve', r, kf, -C1, ang, ALU.mult, ALU.add, [R('kf'), R('ang')], [R('r')])
            self.STT('dve', r, kf, -C2, r, ALU.mult, ALU.add, [R('kf'), R('r')], [R('r')])
            self.TS('dve', rc, r, PI / 2, None, ALU.add, None, [R('r')], [R('rc')])
            self.TS('dve', kf, rc, PI, None, ALU.is_gt, None, [R('rc')], [R('kf')])
            self.STT('dve', rc, kf, -2 * PI, rc, ALU.mult, ALU.add, [R('kf'), R('rc')], [R('rc')])
            self.TS('dve', r, r, -PI, PI, ALU.max, ALU.min, [R('r')], [R('r')])
            self.TS('dve', rc, rc, -PI, PI, ALU.max, ALU.min, [R('rc')], [R('rc')])
            self.ACT(o, r, AF.Sin, [R('r')], [R('o')])
            self.DMA('sp', self.sinT[:, t0:t0 + T], o, [R('o')], [])
            self.ACT(pos, rc, AF.Sin, [R('rc')], [R('pos')])
            self.DMA('sp', self.cosT[:, t0:t0 + T], pos, [R('pos')], [])

    def mla_scratch(self):
        if hasattr(self, 'QnT'):
            return
        NT = self.NT
        d = self.dscr
        self.QnT = d("QnT", [8, 128, NT], BF16)
        self.QrT = d("QrT", [8, 64, NT], BF16)
        self.qnorm = d("qnorm", [8, NT], F32)
        self.KnT = d("KnT", [8, 128, NT], BF16)
        self.KrT = d("KrT", [65, NT], BF16)
        self.Vtok = d("Vtok", [NT, 1024], BF16)

    def o_scratch(self):
        if not hasattr(self, 'oT'):
            self.oT = self.dscr("oT", [8, 128, self.NT], BF16)

    def mla(self, jm, L):
        self.mla_scratch()
        self.o_scratch()
        stop = self.cfg.get('stop', '')
        if stop == 'rope':
            return
        self.mla_proj(jm, L)
        if stop == 'proj':
            return
        self.mla_attn(jm, L)
        if stop == 'attn':
            return
        self.pass_C(self.W['mla_w_out'][jm], L)

    def mla_proj(self, jm, L):
        self.new_phase()
        ar = self.ar
        W = self.W
        self.negkmax = ar.alloc([1], F32)
        self.mla_keep = ar.off
        kmax2 = ar.alloc([1], F32)
        self.epsc = ar.alloc([1], F32)
        eps_c = self.epsc
        self.MEMSET('dve', eps_c, EPS, ['epsc'])
        self.MEMSET('dve', kmax2, 0.0, ['kmax2'])
        win = ar.alloc([8, 448], BF16)
        winr = ar.alloc([8, 64], BF16)
        wuq = ar.alloc([2, 1536], BF16)
        wuqr = ar.alloc([2, 8, 64], BF16)
        wukv = ar.alloc([2048], BF16)
        wkn = ar.alloc([8, 128], BF16)
        wv = ar.alloc([8, 128], BF16)
        gq = ar.alloc([2], F32)
        gkv = ar.alloc([1], F32)
        self.stage_rows = ar.alloc([128], F32)
        onesrow = ar.alloc([T], BF16)
        self.DMA('pool', win, W['mla_w_in'][jm].rearrange("(k p) n -> p k n", p=128), [], ['win'])
        self.DMA('pool', wuq, W['mla_w_uq'][jm].rearrange("(k p) n -> p k n", p=128), [], ['wuq'])
        self.DMA('pool', wukv, W['mla_w_ukv'][jm], [], ['wukv'])
        self.p.dma('sp', gq, W['mla_g_q'][jm].rearrange("(c p) -> p c", p=128), [], ['gq'], allow_slow_non_contiguous=True)
        self.p.dma('sp', gkv, W['mla_g_kv'][jm].rearrange("(c p) -> p c", p=128), [], ['gkv'], allow_slow_non_contiguous=True)
        self.MEMSET('dve', onesrow[0:65, :], 1.0, ['onesrow'])
        wuq4 = wuq.rearrange("p k (h f) -> p k h f", f=192)
        for kc in range(2):
            self.TS('dve', wuqr[:, kc, :, 0:32], wuq4[:, kc, :, 160:192], -1.0, None, ALU.mult, None, ['wuq'], ['wuqr'])
            self.CP('dve', wuqr[:, kc, :, 32:64], wuq4[:, kc, :, 128:160], ['wuq'], ['wuqr'])
        self.TS('dve', winr[:, :, 0:32], win[:, :, 416:448], -1.0, None, ALU.mult, None, ['win'], ['winr'])
        self.CP('dve', winr[:, :, 32:64], win[:, :, 384:416], ['win'], ['winr'])
        wukv3 = wukv.rearrange("p (h f) -> p h f", f=256)
        self.CP('dve', wkn, wukv3[:, :, 0:128], ['wukv'], ['wkn'])
        self.CP('dve', wv, wukv3[:, :, 128:256], ['wukv'], ['wv'])
        wvf = wv.rearrange("p h f -> p (h f)")

        xt = [ar.alloc([8, T], F32) for _ in range(2)]
        sq = ar.alloc([8, T], BF16)
        h = ar.alloc([8, T], BF16)
        rstd = ar.alloc([T], F32)
        tmp = [ar.alloc([T], F32) for _ in range(2)]
        cq = ar.alloc([2, T], F32)
        ckv = ar.alloc([T], F32)
        cqn = ar.alloc([2, T], BF16)
        ckvn = ar.alloc([T], BF16)
        rs2 = ar.alloc([T], F32)
        cos = [ar.alloc([T], F32) for _ in range(2)]
        sin = [ar.alloc([T], F32) for _ in range(2)]
        stb = [ar.alloc([T], BF16) for _ in range(4)]
        sqa = [ar.alloc([T], BF16) for _ in range(2)]
        sqb = [ar.alloc([T], BF16) for _ in range(2)]
        t1 = [ar.alloc([T], F32) for _ in range(2)]
        t2 = [ar.alloc([T], F32) for _ in range(2)]
        qnt = [ar.alloc([T], F32) for _ in range(2)]
        rmx = ar.alloc([1], F32)
        vst = [ar.alloc([1024], BF16) for _ in range(2)]
        xTv = self.xT.rearrange("c p t -> p c t")
        PS = self.ps
        cnt = {'st': 0, 'sq': 0, 't': 0, 'q': 0, 'v': 0, 'pa': 0, 'pb': 0}

        def nxt(k, n):
            v = cnt[k] % n
            cnt[k] += 1
            return v

        def load(t):
            self.DMA('sp', xt[t % 2], xTv[:, :, t * T:(t + 1) * T], [], [('xt%d' % (t % 2), c) for c in range(8)])
            self.DMA('sp', cos[t % 2][0:64, :], self.cosT[:, t * T:(t + 1) * T], [], ['cos%d' % (t % 2)])
            self.DMA('sp', sin[t % 2][0:64, :], self.sinT[:, t * T:(t + 1) * T], [], ['sin%d' % (t % 2)])

        def rope_out(p_x, p_xr, xres, xrres, cs, sn, csr, snr, dst_dram):
            i = nxt('t', 2)
            a, b = t1[i][0:64, :], t2[i][0:64, :]
            self.TT('dve', a, p_x, cs, ALU.mult, [xres, csr], ['t1_%d' % i])
            self.TT('dve', b, p_xr, sn, ALU.mult, [xrres, snr], ['t2_%d' % i])
            si = nxt('st', 4)
            self.TT('pool', stb[si][0:64, :], a, b, ALU.add, ['t1_%d' % i, 't2_%d' % i], ['stb%d' % si])
            self.DMA('sp', dst_dram, stb[si][0:64, :], ['stb%d' % si], [])
            return si

        lvl = int(self.cfg.get('lvl', 9))
        load(0)
        for t in range(self.NTILE):
            t0 = t * T
            s = t0 // self.SEG
            if t + 1 < self.NTILE:
                load(t + 1)
            x = xt[t % 2]
            xr = 'xt%d' % (t % 2)
            cs, sn = cos[t % 2][0:64, :], sin[t % 2][0:64, :]
            csr, snr = 'cos%d' % (t % 2), 'sin%d' % (t % 2)
            self.norm_mod(x, lambda c: (xr, c), sq, lambda c: ('sq', c), PS[6], 'ps6', rstd, tmp, h,
                          lambda c: ('h', c), lambda c: self.Acol(L, 1, c, s), lambda c: self.Bcol(L, 1, c, s))
            hres = [('h', k) for k in range(8)]
            for m in range(2):
                for k in range(8):
                    self.MM(PS[m], win[:, k, m * 128:(m + 1) * 128], h[:, k, :], k == 0, k == 7, ['win', ('h', k)], ['ps%d' % m])
                self.CP('act', cq[:, m, :], PS[m], ['ps%d' % m], [('cq', m)])
                self.ACT(sq[:, m, :], PS[m], AF.Square, ['ps%d' % m], [('sq', m)])
            for m in range(2):
                self.MM(PS[6], self.ones_bf, sq[:, m, :], m == 0, m == 1, ['ones', ('sq', m)], ['ps6'])
            self.ACT(rs2, PS[6], AF.Sqrt, ['ps6', 'epsc'], ['rs2'], bias=eps_c, scale=1.0 / 256)
            self.p.op('dve', lambda e: e.reciprocal(out=rs2, in_=rs2), ['rs2'], ['rs2'])
            for m in range(2):
                self.STT('dve', cqn[:, m, :], cq[:, m, :], gq[:, m:m + 1], rs2, ALU.mult, ALU.mult,
                         [('cq', m), 'gq', 'rs2'], [('cqn', m)])
            for k in range(8):
                self.MM(PS[2], win[:, k, 256:384], h[:, k, :], k == 0, k == 7, ['win', ('h', k)], ['ps2'])
            self.CP('act', ckv, PS[2], ['ps2'], ['ckv'])
            self.ACT(sq[:, 2, :], PS[2], AF.Square, ['ps2'], [('sq', 2)])
            self.MM(PS[6], self.ones_bf, sq[:, 2, :], True, True, ['ones', ('sq', 2)], ['ps6'])
            self.ACT(rs2, PS[6], AF.Sqrt, ['ps6', 'epsc'], ['rs2'], bias=eps_c, scale=1.0 / 128)
            self.p.op('dve', lambda e: e.reciprocal(out=rs2, in_=rs2), ['rs2'], ['rs2'])
            self.STT('dve', ckvn, ckv, gkv[:, 0:1], rs2, ALU.mult, ALU.mult, ['ckv', 'gkv', 'rs2'], ['ckvn'])
            if lvl < 1:
                continue
            for k in range(8):
                self.MM(PS[3][0:64, :], win[:, k, 384:448], h[:, k, :], k == 0, k == 7, ['win', ('h', k)], ['ps3'])
            for k in range(8):
                self.MM(PS[4][0:64, :], winr[:, k, :], h[:, k, :], k == 0, k == 7, ['winr', ('h', k)], ['ps4'])
            i = nxt('sq', 2)
            sr = rope_out(PS[3][0:64, :], PS[4][0:64, :], 'ps3', 'ps4', cs, sn, csr, snr, self.KrT[0:64, t0:t0 + T])
            self.ACT(sqb[i][0:64, :], stb[sr][0:64, :], AF.Square, ['stb%d' % sr], ['sqb%d' % i])
            krsq, krsq_res = sqb[i][0:64, :], 'sqb%d' % i
            self.DMA('sp', self.KrT[64:65, t0:t0 + T], onesrow[64:65, :], ['onesrow'], [])
            if lvl < 2:
                continue
            for hh in range(8):
                b = nxt('pa', 2)
                self.MM(PS[b], wkn[:, hh, :], ckvn, True, True, ['wkn', 'ckvn'], ['ps%d' % b])
                si = nxt('st', 4)
                self.CP('act', stb[si], PS[b], ['ps%d' % b], ['stb%d' % si])
                self.DMA('sp', self.KnT[hh][:, t0:t0 + T], stb[si], ['stb%d' % si], [])
                qi = nxt('sq', 2)
                self.ACT(sqa[qi], PS[b], AF.Square, ['ps%d' % b], ['sqa%d' % qi])
                self.MM(PS[7], self.ones_bf, sqa[qi], True, False, ['ones', 'sqa%d' % qi], ['ps7'])
                self.MM(PS[7], self.ones_bf[0:64, :], krsq, False, True, ['ones', krsq_res], ['ps7'])
                self.p.op('dve', lambda e: e.reduce_max(out=rmx, in_=PS[7], axis=AX.X), ['ps7'], ['rmx'])
                self.TT('dve', kmax2, kmax2, rmx, ALU.max, ['kmax2', 'rmx'], ['kmax2'])
            if lvl < 3:
                continue
            for i4 in range(4):
                vi = nxt('v', 2)
                for half in range(2):
                    b = nxt('pa', 2)
                    self.MM(PS[b], ckvn[:, i4 * 128:(i4 + 1) * 128], wvf[:, half * 512:(half + 1) * 512], True, True,
                            ['ckvn', 'wv'], ['ps%d' % b])
                    self.CP('act' if half == 0 else 'dve', vst[vi][:, half * 512:(half + 1) * 512], PS[b],
                            ['ps%d' % b], [('vst%d' % vi, half)])
                self.DMA('sp', self.Vtok[t0 + i4 * 128:t0 + (i4 + 1) * 128, :], vst[vi],
                         [('vst%d' % vi, 0), ('vst%d' % vi, 1)], [])
            if lvl < 4:
                continue
            for hh in range(8):
                b = nxt('pa', 2)
                for kc in range(2):
                    self.MM(PS[b], wuq[:, kc, hh * 192:hh * 192 + 128], cqn[:, kc, :], kc == 0, kc == 1,
                            ['wuq', ('cqn', kc)], ['ps%d' % b])
                b2 = nxt('pb', 2)
                pr, prr = PS[2 + b2][0:64, :], PS[4 + b2][0:64, :]
                for kc in range(2):
                    self.MM(pr, wuq[:, kc, hh * 192 + 128:hh * 192 + 192], cqn[:, kc, :], kc == 0, kc == 1,
                            ['wuq', ('cqn', kc)], ['ps%d' % (2 + b2)])
                for kc in range(2):
                    self.MM(prr, wuqr[:, kc, hh, :], cqn[:, kc, :], kc == 0, kc == 1,
                            ['wuqr', ('cqn', kc)], ['ps%d' % (4 + b2)])
                si = nxt('st', 4)
                self.CP('act', stb[si], PS[b], ['ps%d' % b], ['stb%d' % si])
                self.DMA('sp', self.QnT[hh][:, t0:t0 + T], stb[si], ['stb%d' % si], [])
                qi = nxt('sq', 2)
                self.ACT(sqa[qi], PS[b], AF.Square, ['ps%d' % b], ['sqa%d' % qi])
                sr = rope_out(pr, prr, 'ps%d' % (2 + b2), 'ps%d' % (4 + b2), cs, sn, csr, snr, self.QrT[hh][:, t0:t0 + T])
                self.ACT(sqb[qi][0:64, :], stb[sr][0:64, :], AF.Square, ['stb%d' % sr], ['sqb%d' % qi])
                self.MM(PS[7], self.ones_bf, sqa[qi], True, False, ['ones', 'sqa%d' % qi], ['ps7'])
                self.MM(PS[7], self.ones_bf[0:64, :], sqb[qi][0:64, :], False, True, ['ones', 'sqb%d' % qi], ['ps7'])
                ni = nxt('q', 2)
                self.ACT(qnt[ni][0:1, :], PS[7][0:1, :], AF.Sqrt, ['ps7'], ['qnt%d' % ni])
                self.DMA('sp', self.qnorm[hh:hh + 1, t0:t0 + T], qnt[ni][0:1, :], ['qnt%d' % ni], [])
        self.ACT(self.negkmax, kmax2, AF.Sqrt, ['kmax2'], ['negkmax'])
        self.TS('dve', self.negkmax, self.negkmax, -1.0, None, ALU.mult, None, ['negkmax'], ['negkmax'])

    def mla_attn(self, jm, L):
        self.p.barrier()
        ar = self.ar
        ar.off = self.mla_keep
        SEG = self.SEG
        zcol = ar.alloc([1], F32)
        self.MEMSET('dve', zcol, 0.0, ['zcol'])
        ones32 = ar.alloc([128], F32)
        self.MEMSET('dve', ones32, 1.0, ['ones32'])
        crossb = self.flags[:, 1:2]
        NKmax = 2 * SEG
        kr = [ar.alloc([NKmax], BF16) for _ in range(2)]
        kn = [ar.alloc([NKmax], BF16) for _ in range(2)]
        vv = [ar.alloc([NKmax // 128, 128], BF16) for _ in range(2)]
        qn = [ar.alloc([T], BF16) for _ in range(2)]
        qr = [ar.alloc([T], BF16) for _ in range(2)]
        qf = [ar.alloc([T], F32) for _ in range(2)]
        pT2 = [ar.alloc([2 * T], BF16) for _ in range(3)]
        acc = [ar.alloc([2 * T], F32) for _ in range(2)]
        rec = [ar.alloc([T], F32) for _ in range(2)]
        ob = [ar.alloc([T], BF16) for _ in range(2)]
        PS = self.ps
        scale = float(192 ** -0.5)
        Vv = self.Vtok.rearrange("(n p) f -> p n f", p=128)
        groups = [(0, 2 * SEG), (2 * SEG, SEG)]
        heads = [(g, hh) for g in range(2) for hh in range(8)]
        items = [(hi, tq) for hi, (g, hh) in enumerate(heads) for tq in range(groups[g][1] // T)]

        def load_head(hi):
            g, hh = heads[hi]
            k0, NK = groups[g]
            hb = hi % 2
            if hh == 0:
                self.DMA('sp', kr[g][0:65, 0:NK], self.KrT[:, k0:k0 + NK], [], ['kr%d' % g])
            self.DMA('sp', kn[hb][:, 0:NK], self.KnT[hh][:, k0:k0 + NK], [], ['kn%d' % hb])
            self.DMA('sp', vv[hb][:, 0:NK // 128, :], Vv[:, k0 // 128:(k0 + NK) // 128, hh * 128:(hh + 1) * 128], [],
                     ['vv%d' % hb])

        def load_q(ii):
            hi, tq = items[ii]
            g, hh = heads[hi]
            q0 = groups[g][0] + tq * T
            qb = ii % 2
            self.DMA('sp', qn[qb], self.QnT[hh][:, q0:q0 + T], [], ['qn%d' % qb])
            self.DMA('sp', qr[qb][0:64, :], self.QrT[hh][:, q0:q0 + T], [], [('qr%d' % qb, 0)])
            self.DMA('sp', qf[qb][64:65, :], self.qnorm[hh:hh + 1, q0:q0 + T], [], ['qf%d' % qb])

        npt = 0
        load_head(0)
        load_q(0)
        for ii, (hi, tq) in enumerate(items):
            g, hh = heads[hi]
            k0, NK = groups[g]
            nsub = NK // 128
            npairs = nsub // 2
            hb = hi % 2
            qb = ii % 2
            q0 = k0 + tq * T
            sq_seg = q0 // SEG
            if tq == 0 and hi + 1 < len(heads):
                load_head(hi + 1)
            if ii + 1 < len(items):
                load_q(ii + 1)
            self.TS('dve', qr[qb][64:65, :], qf[qb][64:65, :], self.negkmax[64:65, 0:1], None, ALU.mult, None,
                    ['qf%d' % qb, 'negkmax'], [('qr%d' % qb, 1)])
            qrres = [('qr%d' % qb, 0), ('qr%d' % qb, 1)]
            Ops, Ores = PS[4 + qb], 'ps%d' % (4 + qb)
            Sps, Sres = PS[6 + qb], 'ps%d' % (6 + qb)

            def scores(pr):
                for j in range(2):
                    kk = 2 * pr + j
                    b = 2 * (pr % 2) + j
                    self.MM(PS[b], kn[hb][:, kk * 128:(kk + 1) * 128], qn[qb], True, False,
                            ['kn%d' % hb, 'qn%d' % qb], ['ps%d' % b])
                    self.MM(PS[b], kr[g][0:65, kk * 128:(kk + 1) * 128], qr[qb][0:65, :], False, True,
                            ['kr%d' % g] + qrres, ['ps%d' % b])

            scores(0)
            for pr in range(npairs):
                if pr + 1 < npairs:
                    scores(pr + 1)
                pi = npt % 3
                npt += 1
                kseg = (k0 + 2 * pr * 128) // SEG
                bias = zcol if kseg == sq_seg else crossb
                b0 = 2 * (pr % 2)
                self.ACT(pT2[pi], self.pp[pr % 2], AF.Exp, ['ps%d' % b0, 'ps%d' % (b0 + 1), 'zcol', 'flags'],
                         ['pT%d' % pi], bias=bias, scale=scale)
                for j in range(2):
                    self.MM(Ops, vv[hb][:, 2 * pr + j, :], pT2[pi][:, j * T:(j + 1) * T], pr == 0 and j == 0,
                            pr == npairs - 1 and j == 1, ['vv%d' % hb, 'pT%d' % pi], [Ores])
                if pr == 0:
                    self.CP('dve', acc[qb], pT2[pi], ['pT%d' % pi], ['acc%d' % qb])
                else:
                    self.TT('dve', acc[qb], acc[qb], pT2[pi], ALU.add, ['acc%d' % qb, 'pT%d' % pi], ['acc%d' % qb])
            self.MM(Sps, ones32, acc[qb][:, 0:T], True, False, ['ones32', 'acc%d' % qb], [Sres])
            self.MM(Sps, ones32, acc[qb][:, T:2 * T], False, True, ['ones32', 'acc%d' % qb], [Sres])
            self.p.op('dve', (lambda r_, s_: (lambda e: e.reciprocal(out=r_, in_=s_)))(rec[qb], Sps),
                      [Sres], ['rec%d' % qb])
            self.TT('dve', ob[qb], Ops, rec[qb], ALU.mult, [Ores, 'rec%d' % qb], ['ob%d' % qb])
            self.DMA('sp', self.oT[hh][:, q0:q0 + T], ob[qb], ['ob%d' % qb], [])

    def pass_C(self, w_out, L):
        self.new_phase()
        ar = self.ar
        wo = ar.alloc([8, D], BF16)
        xt = [ar.alloc([8, T], F32) for _ in range(2)]
        ot = [ar.alloc([8, T], BF16) for _ in range(2)]
        wov = w_out.rearrange("(k p) n -> p k n", p=128)
        for k in range(0, 8, 2):
            self.DMA('pool', wo[:, k:k + 2, :], wov[:, k:k + 2, :], [], [('wo', k), ('wo', k + 1)])
        xTv = self.xT.rearrange("c p t -> p c t")
        oTv = self.oT.rearrange("c p t -> p c t")

        def load(t):
            self.DMA('sp', xt[t % 2], xTv[:, :, t * T:(t + 1) * T], [], [('xt%d' % (t % 2), c) for c in range(8)])
            self.DMA('sp', ot[t % 2], oTv[:, :, t * T:(t + 1) * T], [], ['ot%d' % (t % 2)])

        load(0)
        nd = 0
        for t in range(self.NTILE):
            t0 = t * T
            s = t0 // self.SEG
            if t + 1 < self.NTILE:
                load(t + 1)
            x = xt[t % 2]
            xr = 'xt%d' % (t % 2)
            o = ot[t % 2]
            for mo in range(8):
                b = nd % 2
                nd += 1
                for c in range(8):
                    self.MM(self.ps[b], wo[:, c, mo * 128:(mo + 1) * 128], o[:, c, :], c == 0, c == 7,
                            [('wo', c), 'ot%d' % (t % 2)], ['ps%d' % b])
                self.STT('dve', x[:, mo, :], self.ps[b], self.Gcol(L, 1, mo, s), x[:, mo, :], ALU.mult, ALU.add,
                         ['ps%d' % b, 'Gar', (xr, mo)], [(xr, mo)])
            self.DMA('sp', xTv[:, :, t0:t0 + T], x, [(xr, c) for c in range(8)], [])


    def gla_scratch(self):
        if hasattr(self, 'QinT'):
            return
        NT = self.NT
        d = self.dscr
        self.QinT = [d("QinT%d" % i, [4, 128, NT], BF16) for i in range(2)]
        self.KinT = [d("KinT%d" % i, [4, 128, NT], BF16) for i in range(2)]
        self.Kout = [d("Kout%d" % i, [NT, 512], BF16) for i in range(2)]
        self.Vg = d("Vg", [NT, 1024], BF16)
        self.rT = d("rT", [8, 128, NT], BF16)
        self.obT = d("obT", [8, 128, NT], BF16)

    def gla(self, jg, L):
        self.gla_scratch()
        self.o_scratch()
        self.gla_proj(jg, L)
        self.gla_scan(jg, L)
        self.pass_C(self.W['gla_w_out'][jg], L)

    def gla_proj(self, jg, L):
        self.new_phase()
        ar = self.ar
        W = self.W
        NSUB = self.NT // 128
        self.dec = ar.alloc([2, 4, NSUB], F32)
        self.gla_keep = ar.off
        self.epsc = ar.alloc([1], F32)
        self.MEMSET('dve', self.epsc, EPS, ['epsc'])
        win = ar.alloc([8, 3104], BF16)
        wup = ar.alloc([2, 512], BF16)
        tri = ar.alloc([4, 128], BF16)
        winv = W['gla_w_in'][jg].rearrange("(k p) n -> p k n", p=128)
        for k in range(8):
            self.DMA('pool', win[:, k, :], winv[:, k, :], [], [('win', k)])
        winres = [('win', k) for k in range(8)]
        self.DMA('pool', wup[0:16, :, :], W['gla_w_gate_up'][jg].rearrange("d r n -> r d n"), [], [('wup', 0)])
        self.DMA('pool', wup[16:17, :, :], W['gla_b_gate'][jg:jg + 1], [], [('wup', 1)])
        self.DMA('pool', tri, self.tri_in.rearrange("a p n -> p a n"), [], ['tri'])
        wupres = [('wup', 0), ('wup', 1)]
        xt = [ar.alloc([8, T], F32) for _ in range(2)]
        sq = ar.alloc([8, T], BF16)
        h = ar.alloc([8, T], BF16)
        rstd = ar.alloc([T], F32)
        tmp = [ar.alloc([T], F32) for _ in range(2)]
        qk = ar.alloc([8, T], F32)
        aT = [ar.alloc([T], BF16) for _ in range(2)]
        e32 = [ar.alloc([T], F32) for _ in range(2)]
        labf = [ar.alloc([T], BF16) for _ in range(2)]
        E = [ar.alloc([4, 128], F32) for _ in range(2)]
        Ei = [ar.alloc([4, 128], F32) for _ in range(2)]
        Eo = [ar.alloc([T], F32) for _ in range(2)]
        rst = [ar.alloc([T], BF16) for _ in range(2)]
        kst = [ar.alloc([T], BF16) for _ in range(2)]
        vst = [ar.alloc([1024], BF16) for _ in range(2)]
        qin_st = [ar.alloc([4, T], BF16) for _ in range(2)]
        kin_st = [ar.alloc([4, T], BF16) for _ in range(2)]
        for d_ in range(2):
            self.MEMSET('dve', aT[d_][0:17, :], 1.0, ['aT%d' % d_])
        xTv = self.xT.rearrange("c p t -> p c t")
        PS = self.ps
        cnt = {}

        def nxt(k, n):
            v = cnt.get(k, 0) % n
            cnt[k] = cnt.get(k, 0) + 1
            return v

        def load(t):
            self.DMA('sp', xt[t % 2], xTv[:, :, t * T:(t + 1) * T], [], [('xt%d' % (t % 2), c) for c in range(8)])

        load(0)
        for t in range(self.NTILE):
            t0 = t * T
            s = t0 // self.SEG
            if t + 1 < self.NTILE:
                load(t + 1)
            x = xt[t % 2]
            xr = 'xt%d' % (t % 2)
            self.norm_mod(x, lambda c: (xr, c), sq, lambda c: ('sq', c), PS[7], 'ps7', rstd, tmp, h,
                          lambda c: ('h', c), lambda c: self.Acol(L, 1, c, s), lambda c: self.Bcol(L, 1, c, s))
            for m in range(8):
                b = nxt('pa', 2)
                for k in range(8):
                    self.MM(PS[b], win[:, k, m * 128:(m + 1) * 128], h[:, k, :], k == 0, k == 7,
                            [('win', k), ('h', k)], ['ps%d' % b])
                self.ACT(qk[:, m, :], PS[b], AF.Identity, ['ps%d' % b], [('qk', m)],
                         scale=(float(128 ** -0.5) if m < 4 else 1.0))
            for m in range(8):
                b = nxt('pa', 2)
                for k in range(8):
                    self.MM(PS[b], win[:, k, 2048 + m * 128:2048 + (m + 1) * 128], h[:, k, :], k == 0, k == 7,
                            [('win', k), ('h', k)], ['ps%d' % b])
                ri = nxt('r', 2)
                self.ACT(rst[ri], PS[b], AF.Silu, ['ps%d' % b], ['rst%d' % ri])
                self.DMA('sp', self.rT[m][:, t0:t0 + T], rst[ri], ['rst%d' % ri], [])
            for d_ in range(2):
                for k in range(8):
                    self.MM(PS[2][0:16, :], win[:, k, 3072 + d_ * 16:3088 + d_ * 16], h[:, k, :], k == 0, k == 7,
                            [('win', k), ('h', k)], ['ps2'])
                self.CP('act', aT[d_][0:16, :], PS[2][0:16, :], ['ps2'], ['aT%d' % d_])
            for i4 in range(4):
                sub = slice(i4 * 128, (i4 + 1) * 128)
                n = t * 4 + i4
                for d_ in range(2):
                    self.MM(PS[2 + d_], aT[d_][0:17, sub], wup[0:17, d_, :], True, True, ['aT%d' % d_] + wupres,
                            ['ps%d' % (2 + d_)])
                    self.ACT(e32[d_], PS[2 + d_], AF.Exp, ['ps%d' % (2 + d_)], ['e32_%d' % d_], scale=-1.0)
                    self.ACT(e32[d_], e32[d_], AF.Ln, ['e32_%d' % d_], ['e32_%d' % d_], bias=1.0)
                    self.TS('dve', labf[d_], e32[d_], -1.0 / 16.0, None, ALU.mult, None, ['e32_%d' % d_], ['la%d' % d_])
                for k in range(8):
                    self.MM(PS[4], h[:, k, sub], win[:, k, 512:1024], k == 0, k == 7, [('h', k), ('win', k)], ['ps4'])
                for d_ in range(2):
                    self.MM(PS[5 + d_], tri[:, 2 + d_, :], labf[d_], True, True, ['tri', 'la%d' % d_], ['ps%d' % (5 + d_)])
                    self.ACT(Eo[d_], PS[5 + d_], AF.Exp, ['ps%d' % (5 + d_)], ['Eo%d' % d_])
                    ki = nxt('k', 2)
                    self.TT('dve', kst[ki], PS[4], Eo[d_], ALU.mult, ['ps4', 'Eo%d' % d_], ['kst%d' % ki])
                    self.DMA('sp', self.Kout[d_][t0 + i4 * 128:t0 + (i4 + 1) * 128, :], kst[ki], ['kst%d' % ki], [])
                vi = nxt('v', 2)
                for half in range(2):
                    b = nxt('pa', 2)
                    for k in range(8):
                        self.MM(PS[b], h[:, k, sub], win[:, k, 1024 + half * 512:1024 + (half + 1) * 512], k == 0, k == 7,
                                [('h', k), ('win', k)], ['ps%d' % b])
                    self.CP('act', vst[vi][:, half * 512:(half + 1) * 512], PS[b], ['ps%d' % b], [('vst%d' % vi, half)])
                self.DMA('sp', self.Vg[t0 + i4 * 128:t0 + (i4 + 1) * 128, :], vst[vi],
                         [('vst%d' % vi, 0), ('vst%d' % vi, 1)], [])
                for d_ in range(2):
                    for hh in range(4):
                        self.MM(PS[2 + d_][:, hh * 128:(hh + 1) * 128], labf[d_][:, hh * 128:(hh + 1) * 128], tri[:, d_, :],
                                True, True, ['la%d' % d_, 'tri'], ['ps%d' % (2 + d_)])
                    pv = PS[2 + d_].rearrange("p (a b) -> p a b", b=128)
                    self.ACT(E[d_], pv, AF.Exp, ['ps%d' % (2 + d_)], ['E%d' % d_])
                    self.ACT(Ei[d_], pv, AF.Exp, ['ps%d' % (2 + d_)], ['Ei%d' % d_], scale=-1.0)
                    self.TT('dve', qin_st[d_][:, :, sub], qk[:, 0:4, sub], E[d_], ALU.mult,
                            [('qk', m) for m in range(4)] + ['E%d' % d_], [('qin%d' % d_, i4)])
                    self.TT('dve', kin_st[d_][:, :, sub], qk[:, 4:8, sub], Ei[d_], ALU.mult,
                            [('qk', m) for m in range(4, 8)] + ['Ei%d' % d_], [('kin%d' % d_, i4)])
                    col = 127 if d_ == 0 else 0
                    self.CP('dve', self.dec[:, d_, :, n], E[d_][:, :, col], ['E%d' % d_], ['dec'])
            for d_ in range(2):
                self.DMA('sp', self.QinT[d_].rearrange("c p t -> p c t")[:, :, t0:t0 + T], qin_st[d_],
                         [('qin%d' % d_, i) for i in range(4)], [])
                self.DMA('sp', self.KinT[d_].rearrange("c p t -> p c t")[:, :, t0:t0 + T], kin_st[d_],
                         [('kin%d' % d_, i) for i in range(4)], [])

    def gla_scan(self, jg, L):
        self.p.barrier()
        ar = self.ar
        ar.off = self.gla_keep
        W = self.W
        SEG = self.SEG
        NSUB = self.NT // 128
        SPS = SEG // 128
        self.epsc = ar.alloc([1], F32)
        self.MEMSET('dve', self.epsc, EPS, ['epsc'])
        gn = ar.alloc([2], F32)
        self.p.dma('sp', gn, W['gla_g_norm'][jg].rearrange("(c p) -> p c", p=128), [], ['gn'],
                   allow_slow_non_contiguous=True)
        mask1 = ar.alloc([128], F32)
        maskf = ar.alloc([4, 128], F32)
        S = ar.alloc([4, 256], F32)
        Sbf = ar.alloc([4, 256], BF16)
        qin = [ar.alloc([4, T], BF16) for _ in range(2)]
        kin = [ar.alloc([4, T], BF16) for _ in range(2)]
        kout = [ar.alloc([4, 512], BF16) for _ in range(2)]
        vt = [ar.alloc([4, 1024], BF16) for _ in range(2)]
        obt = [ar.alloc([8, T], BF16) for _ in range(2)]
        rt = [ar.alloc([8, T], BF16) for _ in range(2)]
        ost = [ar.alloc([8, T], BF16) for _ in range(2)]
        o32 = ar.alloc([8, T], F32)
        sq = ar.alloc([8, T], BF16)
        am = [ar.alloc([4, 128], BF16) for _ in range(2)]
        rs = ar.alloc([T], F32)
        tm = [ar.alloc([T], F32) for _ in range(2)]
        PS = self.ps
        link = self.flags[:, 0:1]
        Kv = [self.Kout[d_].rearrange("(n p) f -> p n f", p=128) for d_ in range(2)]
        Vv = self.Vg.rearrange("(n p) f -> p n f", p=128)
        Qv = [self.QinT[d_].rearrange("c p t -> p c t") for d_ in range(2)]
        Knv = [self.KinT[d_].rearrange("c p t -> p c t") for d_ in range(2)]
        obv = self.obT.rearrange("c p t -> p c t")
        rv = self.rT.rearrange("c p t -> p c t")
        ov = self.oT.rearrange("c p t -> p c t")
        na = 0
        for d_ in (1, 0):
            if d_ == 0:
                self.p.barrier()
            self.DMA('sp', mask1, self.tri_in[d_], [], ['mask1'])
            for hh in range(4):
                self.CP('dve', maskf[:, hh, :], mask1, ['mask1'], ['maskf'])
            tiles = list(range(self.NTILE))
            if d_ == 1:
                tiles = tiles[::-1]

            def load(t):
                bi = t % 2
                self.DMA('sp', qin[bi], Qv[d_][:, :, t * T:(t + 1) * T], [], ['qin%d' % bi])
                self.DMA('sp', kin[bi], Knv[d_][:, :, t * T:(t + 1) * T], [], ['kin%d' % bi])
                self.DMA('sp', kout[bi], Kv[d_][:, t * 4:(t + 1) * 4, :], [], ['kout%d' % bi])
                self.DMA('sp', vt[bi], Vv[:, t * 4:(t + 1) * 4, :], [], ['vt%d' % bi])
                if d_ == 0:
                    self.DMA('sp', obt[bi], obv[:, :, t * T:(t + 1) * T], [], ['obt%d' % bi])
                    self.DMA('sp', rt[bi], rv[:, :, t * T:(t + 1) * T], [], ['rt%d' % bi])

            load(tiles[0])
            for ti, t in enumerate(tiles):
                t0 = t * T
                bi = t % 2
                if ti + 1 < len(tiles):
                    load(tiles[ti + 1])
                subs = range(4) if d_ == 0 else range(3, -1, -1)
                for i4 in subs:
                    n = t * 4 + i4
                    sub = slice(i4 * 128, (i4 + 1) * 128)
                    seg_i = n // SPS
                    first = (n % SPS == 0) if d_ == 0 else (n % SPS == SPS - 1)
                    if first:
                        linked = (seg_i == 1) if d_ == 0 else (seg_i == 0)
                        if linked:
                            self.TS('dve', S, S, link, None, ALU.mult, None, ['S', 'flags'], ['S'])
                            self.CP('act', Sbf, S, ['S'], ['Sbf'])
                        else:
                            self.MEMSET('dve', S, 0.0, ['S'])
                            self.MEMSET('dve', Sbf, 0.0, ['Sbf'])
                    ab = na % 2
                    na += 1
                    for hh in range(4):
                        self.MM(PS[ab][:, hh * 128:(hh + 1) * 128], kin[bi][:, hh, sub], qin[bi][:, hh, sub], True, True,
                                ['kin%d' % bi, 'qin%d' % bi], ['ps%d' % ab])
                    self.TT('dve', am[ab], PS[ab].rearrange("p (a b) -> p a b", b=128), maskf, ALU.mult,
                            ['ps%d' % ab, 'maskf'], ['am%d' % ab])
                    for hh in range(4):
                        for vc in range(2):
                            c = hh * 2 + vc
                            bank = 2 + c // 4
                            oo = PS[bank][:, (c % 4) * 128:(c % 4 + 1) * 128]
                            self.MM(oo, vt[bi][:, i4, c * 128:(c + 1) * 128], am[ab][:, hh, :], True, False,
                                    ['vt%d' % bi, 'am%d' % ab], ['ps%d' % bank])
                            self.MM(oo, Sbf[:, hh, vc * 128:(vc + 1) * 128], qin[bi][:, hh, sub], False, True,
                                    ['Sbf', 'qin%d' % bi], ['ps%d' % bank])
                    for hh in range(4):
                        bank = 4 + hh // 2
                        self.MM(PS[bank][:, (hh % 2) * 256:(hh % 2 + 1) * 256], kout[bi][:, i4, hh * 128:(hh + 1) * 128],
                                vt[bi][:, i4, hh * 256:(hh + 1) * 256], True, True, ['kout%d' % bi, 'vt%d' % bi],
                                ['ps%d' % bank])
                    for half in range(2):
                        pv = PS[2 + half].rearrange("p (a b) -> p a b", b=128)
                        if d_ == 1:
                            self.CP('act', ost[bi][:, half * 4:(half + 1) * 4, sub], pv, ['ps%d' % (2 + half)],
                                    [('ost%d' % bi, i4, half)])
                        else:
                            self.TT('dve', o32[:, half * 4:(half + 1) * 4, sub], pv, obt[bi][:, half * 4:(half + 1) * 4, sub],
                                    ALU.add, ['ps%d' % (2 + half), 'obt%d' % bi], [('o32', i4, half)])
                    for hh in range(4):
                        bank = 4 + hh // 2
                        self.STT('dve', S[:, hh, :], S[:, hh, :], self.dec[:, d_, hh, n:n + 1],
                                 PS[bank][:, (hh % 2) * 256:(hh % 2 + 1) * 256], ALU.mult, ALU.add,
                                 ['S', 'dec', 'ps%d' % bank], ['S'])
                    self.CP('act', Sbf, S, ['S'], ['Sbf'])
                if d_ == 1:
                    self.DMA('sp', obv[:, :, t0:t0 + T], ost[bi],
                             [('ost%d' % bi, i, hf) for i in range(4) for hf in range(2)], [])
                else:
                    ores = [('o32', i, hf) for i in range(4) for hf in range(2)]
                    for c in range(8):
                        self.ACT(sq[:, c, :], o32[:, c, :], AF.Square, ores, [('sq', c)])
                    for hh in range(4):
                        pb = 6 + hh % 2
                        for vc in range(2):
                            self.MM(PS[pb], self.ones_bf, sq[:, hh * 2 + vc, :], vc == 0, vc == 1,
                                    ['ones', ('sq', hh * 2 + vc)], ['ps%d' % pb])
                        self.ACT(rs, PS[pb], AF.Sqrt, ['ps%d' % pb, 'epsc'], ['rs'], bias=self.epsc, scale=1.0 / 256)
                        self.p.op('dve', lambda e: e.reciprocal(out=rs, in_=rs), ['rs'], ['rs'])
                        for vc in range(2):
                            c = hh * 2 + vc
                            self.STT('dve', tm[vc], o32[:, c, :], gn[:, vc:vc + 1], rs, ALU.mult, ALU.mult,
                                     ores + ['gn', 'rs'], ['tm%d' % vc])
                            self.TT('pool', ost[bi][:, c, :], tm[vc], rt[bi][:, c, :], ALU.mult,
                                    ['tm%d' % vc, 'rt%d' % bi], [('ostc%d' % bi, c)])
                    self.DMA('sp', ov[:, :, t0:t0 + T], ost[bi], [('ostc%d' % bi, c) for c in range(8)], [])

    def build(self):
        self.prologue()
        self.mixsel = self.cfg.get('mixsel', 1)
        if self.mixers and self.mixsel in (1, 3):
            self.rope_tables()
        self.pass_T()
        for L in range(self.depth):
            self.pass_F(L, 0)
            if self.mixers:
                if L % 2 == 0:
                    if self.mixsel in (1, 2):
                        self.gla(L // 2, L)
                else:
                    if self.mixsel in (1, 3):
                        self.mla(L // 2, L)
            self.pass_F(L, 1)
        self.pass_O()
        self.p.emit()
        return self.nc


def host_consts():
    ident = np.eye(128, dtype=np.float32)
    i = np.arange(128)
    tri = np.zeros((4, 128, 128), np.float32)
    tri[0] = (i[:, None] <= i[None, :])
    tri[1] = (i[:, None] >= i[None, :])
    tri[2] = (i[:, None] > i[None, :])
    tri[3] = (i[:, None] < i[None, :])
    half = np.arange(32, dtype=np.float32)
    inv = (1.0 / (10000.0 ** (np.arange(0, 64, 2, dtype=np.float32) / 64.0))).astype(np.float32)
    invf = np.concatenate([inv, inv]).reshape(64, 1).astype(np.float32)
    return ident, tri, invf


def core_inputs(xsegs, csegs, link, pos_off1, weights):
    SEG = xsegs[0].shape[0]
    ident, tri, invf = host_consts()
    x = np.ascontiguousarray(np.concatenate(xsegs, axis=0))
    c = np.stack(csegs, axis=0)
    crows = np.ascontiguousarray(c.reshape(3, 8, 128).transpose(1, 0, 2).reshape(24, 128))
    flags = np.zeros((128, 4), np.float32)
    flags[:, 0] = link
    flags[:, 1] = 0.0 if link else -30000.0
    ar = np.arange(SEG, dtype=np.float32)
    pos = np.concatenate([ar, ar + pos_off1, ar]).reshape(1, 3 * SEG).astype(np.float32)
    m = {"x": x, "crows": crows, "flags": flags, "pos": pos, "ident": ident, "tri": tri, "invf": invf}
    m.update(weights)
    return m


WNAMES = ['ada_w', 'ada_b', 'norm_g', 'ffn_w_gate', 'ffn_w_up', 'ffn_w_down', 'gla_w_in', 'gla_w_gate_up',
          'gla_b_gate', 'gla_g_norm', 'gla_w_out', 'mla_w_in', 'mla_g_q', 'mla_g_kv', 'mla_w_uq', 'mla_w_ukv',
          'mla_w_out', 'final_ada_w', 'final_ada_b', 'final_g']


def kernel(**inputs):
    inp = {k: np.asarray(v) for k, v in inputs.items()}
    weights = {k: np.ascontiguousarray(inp[k], dtype=np.float32) for k in WNAMES}
    xp, xs_, cp, cs = inp['x_prompt'], inp['x_sample'], inp['c_prompt'], inp['c_sample']
    SEG = xp.shape[1]
    in_maps = []
    for core in range(8):
        if core < 4:
            xsegs = [xs_[core, 0:SEG], xs_[core, SEG:2 * SEG], xp[core]]
            csegs = [cs[core], cs[core], cp[core]]
            in_maps.append(core_inputs(xsegs, csegs, 1.0, float(SEG), weights))
        else:
            b = 4 + 3 * (core - 4)
            xsegs = [xp[b], xp[b + 1], xp[b + 2]]
            csegs = [cp[b], cp[b + 1], cp[b + 2]]
            in_maps.append(core_inputs(xsegs, csegs, 0.0, 0.0, weights))
    nc = K({'SEG': SEG}).build()
    res = run_bass_kernel_spmd(nc, in_maps, core_ids=list(range(8)))
    y_prompt = np.zeros(xp.shape, np.float32)
    y_sample = np.zeros(xs_.shape, np.float32)
    for core in range(8):
        y = np.asarray(res.results[core]["y"], dtype=np.float32)
        if core < 4:
            y_sample[core, 0:SEG] = y[0:SEG]
            y_sample[core, SEG:2 * SEG] = y[SEG:2 * SEG]
            y_prompt[core] = y[2 * SEG:]
        else:
            b = 4 + 3 * (core - 4)
            for i in range(3):
                y_prompt[b + i] = y[i * SEG:(i + 1) * SEG]
    return (y_prompt, y_sample)
```

```python
import numpy as np
import concourse.bass as bass
import concourse.mybir as mybir
from concourse.bass_utils import run_bass_kernel_spmd

F32 = mybir.dt.float32
BF16 = mybir.dt.bfloat16
AF = mybir.ActivationFunctionType
ALU = mybir.AluOpType
AX = mybir.AxisListType

SEM_LIMIT = 30000
DMA_SLOTS = 6
SAME_ENGINE_SYNC = True


class Prog:
    def __init__(self, nc):
        self.nc = nc
        self.ops = []

    def op(self, eng, fn, reads=(), writes=(), dma=False):
        self.ops.append((eng, fn, tuple(reads), tuple(writes), dma))

    def barrier(self):
        self.ops.append(('barrier', None, (), (), False))

    def dma(self, q, out, in_, reads=(), writes=(), **kw):
        self.op(q, lambda e: e.dma_start(out=out, in_=in_, **kw), reads, writes, dma=True)

    def emit(self):
        nc = self.nc
        ops = self.ops
        n = len(ops)
        last_w = {}
        readers = {}
        deps = [None] * n
        last_eng = {}
        recent_dma = {}
        pend = {}
        for i, (eng, fn, rd, wr, isdma) in enumerate(ops):
            if eng == 'barrier':
                snap = set(last_eng.values())
                for q, l in recent_dma.items():
                    snap.update(l[-DMA_SLOTS:])
                for en in ('pe', 'act', 'dve', 'pool', 'sp'):
                    pend.setdefault(en, set()).update(snap)
                last_w = {}
                readers = {}
                deps[i] = []
                continue
            d = set()
            if eng in pend:
                d.update(pend.pop(eng))
            last_eng[eng] = i
            if isdma:
                recent_dma.setdefault(eng, []).append(i)
            for r in rd:
                j = last_w.get(r)
                if j is not None:
                    d.add(j)
            for w in wr:
                j = last_w.get(w)
                if j is not None:
                    d.add(j)
                d.update(readers.get(w, {}).values())
            d.discard(i)
            for r in rd:
                rl = readers.setdefault(r, {})
                rl[i if isdma else eng] = i
            for w in wr:
                last_w[w] = i
                readers[w] = {}
            dd = []
            for j in d:
                je, _, _, _, jd = ops[j]
                if not jd and je == eng:
                    if eng == 'pe' or not SAME_ENGINE_SYNC:
                        continue
                dd.append(j)
            deps[i] = sorted(dd)
        needed = set()
        for d in deps:
            needed.update(d)
        sig = {}
        cnt = {}
        dma_hist = {}
        sems = {}

        def getsem(key):
            if key not in sems:
                sems[key] = nc.alloc_semaphore("s_%s" % "_".join(str(x) for x in key))
            return sems[key]

        extra_dep = {}
        for i, (eng, fn, rd, wr, isdma) in enumerate(ops):
            if eng == 'barrier':
                continue
            if isdma:
                h = dma_hist.setdefault(eng, [])
                k = len(h)
                slot = k % DMA_SLOTS
                r = k // DMA_SLOTS
                ep = r // (SEM_LIMIT // 16)
                val = 16 * (r % (SEM_LIMIT // 16) + 1)
                sig[i] = (('d', eng, slot, ep), val, k)
                if k >= DMA_SLOTS:
                    extra_dep[i] = h[k - DMA_SLOTS]
                h.append(i)
            elif i in needed:
                k = cnt.get(eng, 0)
                cnt[eng] = k + 1
                sig[i] = (('c', eng, k // SEM_LIMIT), k % SEM_LIMIT + 1, k)
        self.nsig = dict(cnt)
        per_eng = {}
        for i, o in enumerate(ops):
            if o[0] != 'barrier':
                per_eng.setdefault(o[0], []).append(i)

        def emit_engine(ename, e):
            waited_k = {}
            waited_s = {}
            nwait = 0
            for i in per_eng.get(ename, ()):
                eng, fn, rd, wr, isdma = ops[i]
                dl = list(deps[i])
                if i in extra_dep:
                    dl.append(extra_dep[i])
                for j in dl:
                    key, val, k = sig[j]
                    if key[0] == 'c':
                        if waited_k.get(key[1], -1) >= k:
                            continue
                        waited_k[key[1]] = k
                    else:
                        if waited_s.get(key, 0) >= val:
                            continue
                        waited_s[key] = val
                    e.wait_ge(getsem(key), val)
                    nwait += 1
                ins = fn(e)
                if i in sig:
                    key, val, k = sig[i]
                    ins.then_inc(getsem(key), 16 if isdma else 1)
            h = dma_hist.get(ename, [])
            for j in h[-DMA_SLOTS:]:
                key, val, k = sig[j]
                if waited_s.get(key, 0) >= val:
                    continue
                waited_s[key] = val
                e.wait_ge(getsem(key), val)
            self.nwait = getattr(self, 'nwait', 0) + nwait

        with nc.Block() as block:
            @block.tensor
            def _(e):
                emit_engine('pe', e)

            @block.scalar
            def _(e):
                emit_engine('act', e)

            @block.vector
            def _(e):
                emit_engine('dve', e)

            @block.gpsimd
            def _(e):
                emit_engine('pool', e)

            @block.sync
            def _(e):
                emit_engine('sp', e)


D = 1024
DFF = 2816
NCH = 8
FCH = 22
T = 512
EPS = 1e-6
NMOD = 9


def prod(l):
    r = 1
    for v in l:
        r *= int(v)
    return r


class Arena:
    def __init__(self, t, nbytes):
        self.t = t
        self.nbytes = nbytes
        self.off = 0

    def alloc(self, free_shape, dt, parts=128):
        n = prod(free_shape)
        sz = 4 if dt == F32 else 2
        nb = (n * sz + 31) // 32 * 32
        assert self.off + nb <= self.nbytes, ("arena overflow", self.off, nb, self.nbytes)
        w0 = self.off // 4
        ap = self.t[0:parts, w0:w0 + (n * sz + 3) // 4]
        if dt != F32:
            ap = ap.bitcast(dt)
        if len(free_shape) == 2:
            ap = ap.rearrange("p (a b) -> p a b", b=int(free_shape[1]))
        elif len(free_shape) == 3:
            ap = ap.rearrange("p (a b c) -> p a b c", b=int(free_shape[1]), c=int(free_shape[2]))
        self.off += nb
        return ap


class K:
    def __init__(self, cfg):
        self.cfg = cfg
        self.SEG = cfg['SEG']
        self.NT = 3 * self.SEG
        self.NTILE = self.NT // T
        self.depth = cfg.get('depth', 4)
        self.mixers = cfg.get('mixers', True)
        nc = self.nc = bass.Bass("TRN2", target_bir_lowering=False)
        self.p = Prog(nc)
        NT = self.NT

        def din(name, shape, dt=F32):
            return nc.dram_tensor(name, list(shape), dt, kind="ExternalInput").ap()

        def dscr(name, shape, dt=F32):
            return nc.dram_tensor(name, list(shape), dt, kind="Internal").ap()

        self.x_in = din("x", [NT, D])
        self.c_in = din("crows", [24, 128])
        self.flags_in = din("flags", [128, 4])
        self.pos_in = din("pos", [1, NT])
        self.ident_in = din("ident", [128, 128])
        self.tri_in = din("tri", [4, 128, 128])
        self.invf_in = din("invf", [64, 1])
        W = self.W = {}
        W['ada_w'] = din("ada_w", [4, D, NMOD * D])
        W['ada_b'] = din("ada_b", [4, NMOD * D])
        W['norm_g'] = din("norm_g", [4, 3, D])
        W['ffn_w_gate'] = din("ffn_w_gate", [4, 2, D, DFF])
        W['ffn_w_up'] = din("ffn_w_up", [4, 2, D, DFF])
        W['ffn_w_down'] = din("ffn_w_down", [4, 2, DFF, D])
        W['gla_w_in'] = din("gla_w_in", [2, D, 3104])
        W['gla_w_gate_up'] = din("gla_w_gate_up", [2, 2, 16, 512])
        W['gla_b_gate'] = din("gla_b_gate", [2, 2, 512])
        W['gla_g_norm'] = din("gla_g_norm", [2, 256])
        W['gla_w_out'] = din("gla_w_out", [2, D, D])
        W['mla_w_in'] = din("mla_w_in", [2, D, 448])
        W['mla_g_q'] = din("mla_g_q", [2, 256])
        W['mla_g_kv'] = din("mla_g_kv", [2, 128])
        W['mla_w_uq'] = din("mla_w_uq", [2, 256, 1536])
        W['mla_w_ukv'] = din("mla_w_ukv", [2, 128, 2048])
        W['mla_w_out'] = din("mla_w_out", [2, D, D])
        W['final_ada_w'] = din("final_ada_w", [D, 2 * D])
        W['final_ada_b'] = din("final_ada_b", [2 * D])
        W['final_g'] = din("final_g", [D])
        self.y_out = nc.dram_tensor("y", [NT, D], F32, kind="ExternalOutput").ap()
        self.xT = dscr("xT", [NCH, 128, NT])
        self.dscr = dscr

        nbytes = (nc.sbuf_bytes_remaining - 256) // 32 * 32
        self.arena_t = nc.alloc_sbuf_tensor("arena", [128, nbytes // 4], F32)
        self.ar = Arena(self.arena_t, nbytes)
        self.pp = [nc.alloc_psum_tensor("pp%d" % i, [128, 1024], F32)[:] for i in range(4)]
        self.ps = [self.pp[i // 2][:, (i % 2) * 512:(i % 2 + 1) * 512] for i in range(8)]

    def MM(self, out, lhsT, rhs, start, stop, rd, wr):
        self.p.op('pe', lambda e: e.matmul(out, lhsT, rhs, start=start, stop=stop), rd, wr)

    def TR(self, out, in_, ident, rd, wr):
        self.p.op('pe', lambda e: e.transpose(out, in_, ident), rd, wr)

    def ACT(self, out, in_, func, rd, wr, bias=0.0, scale=1.0):
        self.p.op('act', lambda e: e.activation(out=out, in_=in_, func=func, bias=bias, scale=scale), rd, wr)

    def TS(self, eng, out, in0, s1, s2, op0, op1, rd, wr):
        if s2 is None:
            self.p.op(eng, lambda e: e.tensor_scalar(out=out, in0=in0, scalar1=s1, scalar2=None, op0=op0), rd, wr)
        else:
            self.p.op(eng, lambda e: e.tensor_scalar(out=out, in0=in0, scalar1=s1, scalar2=s2, op0=op0, op1=op1), rd, wr)

    def STT(self, eng, out, in0, scalar, in1, op0, op1, rd, wr):
        self.p.op(eng, lambda e: e.scalar_tensor_tensor(out=out, in0=in0, scalar=scalar, in1=in1, op0=op0, op1=op1), rd, wr)

    def TT(self, eng, out, in0, in1, op, rd, wr):
        self.p.op(eng, lambda e: e.tensor_tensor(out=out, in0=in0, in1=in1, op=op), rd, wr)

    def CP(self, eng, out, in_, rd, wr):
        if eng == 'act':
            self.p.op('act', lambda e: e.copy(out=out, in_=in_), rd, wr)
        else:
            self.p.op(eng, lambda e: e.tensor_copy(out=out, in_=in_), rd, wr)

    def MEMSET(self, eng, ap, val, wr):
        self.p.op(eng, lambda e: e.memset(ap, val), (), wr)

    def DMA(self, q, out, in_, rd, wr):
        self.p.dma(q, out, in_, rd, wr)

    def load_cols(self, rows_ap, R, dest, dest_res, tag):
        st = self.stage_rows
        self.DMA('sp', st[0:R, :], rows_ap, [], ['stage_rows'])
        self.TR(self.ps[7][:, 0:R], st[0:R, :], self.ident[0:R, 0:R], ['stage_rows', 'ident'], ['ps7'])
        self.CP('dve', dest, self.ps[7][:, 0:R], ['ps7'], [dest_res])

    def prologue(self):
        ar = self.ar
        W = self.W
        depth = self.depth
        self.ident = ar.alloc([128], F32)
        self.ones_bf = ar.alloc([128], BF16)
        self.flags = ar.alloc([4], F32)
        self.modT = ar.alloc([4, 216], F32)
        self.Aar = ar.alloc([4, 3, 24], F32)
        self.Gar = ar.alloc([4, 3, 24], F32)
        self.finmod = ar.alloc([48], F32)
        self.Afin = ar.alloc([24], F32)
        self.normg = ar.alloc([96], F32)
        self.fing = ar.alloc([8], F32)
        self.cact = ar.alloc([24], BF16)
        self.persist_mark = ar.off
        self.stage_rows = ar.alloc([128], F32)
        cT = ar.alloc([24], F32)
        adab = ar.alloc([72], F32)
        finb = ar.alloc([16], F32)
        wblk = [ar.alloc([8, 1024], BF16) for _ in range(2)]
        self.DMA('sp', self.ident, self.ident_in, [], ['ident'])
        self.DMA('sp', self.flags, self.flags_in, [], ['flags'])
        self.MEMSET('dve', self.ones_bf, 1.0, ['ones'])
        self.load_cols(self.c_in, 24, cT, 'cT', 'c')
        self.ACT(self.cact, cT, AF.Silu, ['cT'], ['cact'])
        self.load_cols(W['norm_g'].rearrange("l j (c p) -> (l j c) p", p=128), 96, self.normg, 'normg', 'ng')
        self.load_cols(W['final_g'].rearrange("(c p) -> c p", p=128), 8, self.fing, 'fing', 'fg')
        self.load_cols(W['final_ada_b'].rearrange("(c p) -> c p", p=128), 16, finb, 'finb', 'fb')
        nblk = 0

        def mod_block(wsrc, bias_t, bias_res, mc0, dest, dest_res):
            nonlocal nblk
            wb = wblk[nblk % 2]
            wres = 'wblk%d' % (nblk % 2)
            nblk += 1
            self.DMA('pool', wb, wsrc, [], [wres])
            pst = self.ps[nblk % 2]
            psr = 'ps%d' % (nblk % 2)
            for mcl in range(8):
                for k in range(8):
                    self.MM(pst[:, mcl * 3:(mcl + 1) * 3], wb[:, k, mcl * 128:(mcl + 1) * 128],
                            self.cact[:, k * 3:(k + 1) * 3], k == 0, k == 7, [wres, 'cact'], [psr])
            for mcl in range(8):
                mc = mc0 + mcl
                self.TS('dve', dest[:, mc * 3:(mc + 1) * 3], pst[:, mcl * 3:(mcl + 1) * 3],
                        bias_t[:, mc:mc + 1], None, ALU.add, None, [psr, bias_res], [dest_res])

        for L in range(depth):
            self.load_cols(W['ada_b'][L].rearrange("(c p) -> c p", p=128), 72, adab, 'adab', 'ab')
            wv = W['ada_w'][L].rearrange("(k p) n -> p k n", p=128)
            for blk in range(9):
                mod_block(wv[:, :, blk * 1024:(blk + 1) * 1024], adab, 'adab', blk * 8, self.modT[:, L, :], 'modT')
            for j in range(3):
                for c in range(8):
                    col = ((3 * j + 1) * 8 + c) * 3
                    self.TS('dve', self.Aar[:, L, j, c * 3:(c + 1) * 3], self.modT[:, L, col:col + 3], 1.0,
                            self.normg[:, (L * 3 + j) * 8 + c:(L * 3 + j) * 8 + c + 1], ALU.add, ALU.mult,
                            ['modT', 'normg'], ['Aar'])
                col = ((3 * j + 2) * 8) * 3
                self.TS('dve', self.Gar[:, L, j, :], self.modT[:, L, col:col + 24], 1.0 if j == 1 else 0.5, None,
                        ALU.mult, None, ['modT'], ['Gar'])
        wv = W['final_ada_w'].rearrange("(k p) n -> p k n", p=128)
        for blk in range(2):
            mod_block(wv[:, :, blk * 1024:(blk + 1) * 1024], finb, 'finb', blk * 8, self.finmod, 'finmod')
        for c in range(8):
            self.TS('dve', self.Afin[:, c * 3:(c + 1) * 3], self.finmod[:, (8 + c) * 3:(8 + c) * 3 + 3], 1.0,
                    self.fing[:, c:c + 1], ALU.add, ALU.mult, ['finmod', 'fing'], ['Afin'])

    def Acol(self, L, j, c, s):
        return self.Aar[:, L, j, c * 3 + s:c * 3 + s + 1]

    def Bcol(self, L, j, c, s):
        col = ((3 * j) * 8 + c) * 3 + s
        return self.modT[:, L, col:col + 1]

    def Gcol(self, L, j, c, s):
        return self.Gar[:, L, j, c * 3 + s:c * 3 + s + 1]

    def new_phase(self):
        self.p.barrier()
        self.ar.off = self.persist_mark

    def norm_mod(self, xt, xres, sq, sqres, ssq_ps, ssq_res, rstd, tmp, hout, hres, Acol, Bcol):
        self.norm_A(xt, xres, sq, sqres)
        self.norm_B(sq, sqres, ssq_ps, ssq_res)
        self.norm_C(xt, xres, ssq_ps, ssq_res, rstd, tmp, hout, hres, Acol, Bcol)

    def norm_A(self, xt, xres, sq, sqres):
        for c in range(8):
            self.ACT(sq[:, c, :], xt[:, c, :], AF.Square, [xres(c)], [sqres(c)])

    def norm_B(self, sq, sqres, ssq_ps, ssq_res):
        for c in range(8):
            self.MM(ssq_ps, self.ones_bf, sq[:, c, :], c == 0, c == 7, ['ones', sqres(c)], [ssq_res])

    def norm_C(self, xt, xres, ssq_ps, ssq_res, rstd, tmp, hout, hres, Acol, Bcol):
        self.ACT(rstd, ssq_ps, AF.Sqrt, [ssq_res, 'epsc'], ['rstd'], bias=self.epsc, scale=1.0 / D)
        self.p.op('dve', lambda e: e.reciprocal(out=rstd, in_=rstd), ['rstd'], ['rstd'])
        for c in range(8):
            tm = tmp[c % len(tmp)]
            tr = 'tmp%d' % (c % len(tmp))
            self.STT('dve', tm, xt[:, c, :], Acol(c), rstd, ALU.mult, ALU.mult, [xres(c), 'rstd', 'Aar'], [tr])
            self.ACT(hout[:, c, :], tm, AF.Identity, [tr, 'modT'], [hres(c)], bias=Bcol(c), scale=1.0)

    def pass_T(self):
        self.new_phase()
        ar = self.ar
        xin = [ar.alloc([D], F32) for _ in range(4)]
        xs = [ar.alloc([8, T], F32) for _ in range(2)]
        xTv = self.xT.rearrange("c p t -> p c t")
        n = 0
        for t in range(self.NTILE):
            t0 = t * T
            for i in range(4):
                self.DMA('sp', xin[i], self.x_in[t0 + i * 128:t0 + (i + 1) * 128, :], [], ['xin%d' % i])
            xo = xs[t % 2]
            xr = 'xs%d' % (t % 2)
            for c in range(8):
                b = n % 2
                n += 1
                for i in range(4):
                    self.TR(self.ps[b][:, i * 128:(i + 1) * 128], xin[i][:, c * 128:(c + 1) * 128], self.ident,
                            ['xin%d' % i, 'ident'], ['ps%d' % b])
                self.CP('dve' if c % 2 == 0 else 'act', xo[:, c, :], self.ps[b], ['ps%d' % b], [(xr, c)])
            self.DMA('sp', xTv[:, :, t0:t0 + T], xo, [(xr, c) for c in range(8)], [])

    def pass_O(self):
        self.new_phase()
        ar = self.ar
        self.epsc = ar.alloc([1], F32)
        self.MEMSET('dve', self.epsc, EPS, ['epsc'])
        xt = [ar.alloc([8, T], F32) for _ in range(2)]
        sq = ar.alloc([8, T], BF16)
        rstd = ar.alloc([T], F32)
        tmp = [ar.alloc([T], F32) for _ in range(2)]
        hf = ar.alloc([8, T], F32)
        yo = [ar.alloc([D], F32) for _ in range(2)]
        xTv = self.xT.rearrange("c p t -> p c t")
        n = 0
        ny = 0
        for t in range(self.NTILE):
            t0 = t * T
            s = t0 // self.SEG
            x = xt[t % 2]
            xr = 'xt%d' % (t % 2)
            self.DMA('sp', x, xTv[:, :, t0:t0 + T], [], [(xr, c) for c in range(8)])
            self.norm_mod(x, lambda c: (xr, c), sq, lambda c: ('sq', c), self.ps[6], 'ps6', rstd, tmp, hf,
                          lambda c: ('hf', c),
                          lambda c: self.Afin[:, c * 3 + s:c * 3 + s + 1],
                          lambda c: self.finmod[:, c * 3 + s:c * 3 + s + 1])
            for i in range(4):
                y = yo[ny % 2]
                yr = 'yo%d' % (ny % 2)
                ny += 1
                for half in range(2):
                    b = n % 2
                    n += 1
                    for c4 in range(4):
                        c = half * 4 + c4
                        self.TR(self.ps[b][:, c4 * 128:(c4 + 1) * 128], hf[:, c, i * 128:(i + 1) * 128], self.ident,
                                [('hf', c), 'ident'], ['ps%d' % b])
                    self.CP('dve' if half == 0 else 'act', y[:, half * 512:(half + 1) * 512], self.ps[b],
                            ['ps%d' % b], [(yr, half)])
                self.DMA('sp', self.y_out[t0 + i * 128:t0 + (i + 1) * 128, :], y, [(yr, 0), (yr, 1)], [])

    def ffn_weights(self, L, f):
        ar = self.ar
        W = self.W
        wg = ar.alloc([8, DFF], BF16)
        wu = ar.alloc([8, DFF], BF16)
        wd = ar.alloc([FCH, D], BF16)
        wgv = W['ffn_w_gate'][L, f].rearrange("(k p) n -> p k n", p=128)
        wuv = W['ffn_w_up'][L, f].rearrange("(k p) n -> p k n", p=128)
        wdv = W['ffn_w_down'][L, f].rearrange("(k p) n -> p k n", p=128)
        for k in range(8):
            self.DMA('pool', wg[:, k, :], wgv[:, k, :], [], [('wg', k)])
            self.DMA('pool', wu[:, k, :], wuv[:, k, :], [], [('wu', k)])
        for k in range(0, FCH, 2):
            self.DMA('pool', wd[:, k:k + 2, :], wdv[:, k:k + 2, :], [], [('wd', k), ('wd', k + 1)])
        return wg, wu, wd

    def pass_F(self, L, f, preloaded=None):
        self.new_phase()
        ar = self.ar
        j = 0 if f == 0 else 2
        if preloaded is None:
            wg, wu, wd = self.ffn_weights(L, f)
        else:
            wg, wu, wd = preloaded
            ar.off = self.ffn_w_end
        self.epsc = ar.alloc([1], F32)
        self.MEMSET('dve', self.epsc, EPS, ['epsc'])
        xt = [ar.alloc([8, T], F32) for _ in range(2)]
        h = ar.alloc([8, T], BF16)
        act = ar.alloc([FCH, T], BF16)
        rstd = ar.alloc([T], F32)
        tmp = [ar.alloc([T], F32) for _ in range(1)]
        sg = [ar.alloc([T], BF16) for _ in range(2)]
        xTv = self.xT.rearrange("c p t -> p c t")

        def load(t):
            self.DMA('sp', xt[t % 2], xTv[:, :, t * T:(t + 1) * T], [], [('xt%d' % (t % 2), c) for c in range(8)])

        def xr_(t):
            return lambda c: ('xt%d' % (t % 2), c)

        hres = lambda c: ('h', c)

        def normC(t):
            s_ = (t * T) // self.SEG
            self.norm_C(xt[t % 2], xr_(t), self.ps[6], 'ps6', rstd, tmp, h, hres,
                        lambda c: self.Acol(L, j, c, s_), lambda c: self.Bcol(L, j, c, s_))

        load(0)
        self.norm_A(xt[0], xr_(0), h, hres)
        self.norm_B(h, hres, self.ps[6], 'ps6')
        normC(0)
        ng = 0
        nd = 0
        for t in range(self.NTILE):
            t0 = t * T
            s = t0 // self.SEG
            nxt_ = t + 1 < self.NTILE
            if nxt_:
                load(t + 1)
            x = xt[t % 2]
            xr = 'xt%d' % (t % 2)
            for m in range(FCH):
                b = ng % 2
                ng += 1
                pg, pgr = self.ps[b], 'ps%d' % b
                pu, pur = self.ps[2 + b], 'ps%d' % (2 + b)
                for k in range(8):
                    self.MM(pg, wg[:, k, m * 128:(m + 1) * 128], h[:, k, :], k == 0, k == 7, [('wg', k), ('h', k)], [pgr])
                for k in range(8):
                    self.MM(pu, wu[:, k, m * 128:(m + 1) * 128], h[:, k, :], k == 0, k == 7, [('wu', k), ('h', k)], [pur])
                self.ACT(sg[b], pg, AF.Silu, [pgr], ['sg%d' % b])
                self.TT('dve', act[:, m, :], sg[b], pu, ALU.mult, ['sg%d' % b, pur], [('act', m)])
            if nxt_:
                self.norm_A(xt[(t + 1) % 2], xr_(t + 1), h, hres)
            for mo in range(8):
                if mo == 4 and nxt_:
                    self.norm_B(h, hres, self.ps[6], 'ps6')
                    normC(t + 1)
                b = nd % 2
                nd += 1
                pd, pdr = self.ps[4 + b], 'ps%d' % (4 + b)
                for k in range(FCH):
                    self.MM(pd, wd[:, k, mo * 128:(mo + 1) * 128], act[:, k, :], k == 0, k == FCH - 1,
                            [('wd', k), ('act', k)], [pdr])
                self.STT('dve', x[:, mo, :], pd, self.Gcol(L, j, mo, s), x[:, mo, :], ALU.mult, ALU.add,
                         [pdr, 'Gar', (xr, mo)], [(xr, mo)])
            self.DMA('sp', xTv[:, :, t0:t0 + T], x, [(xr, c) for c in range(8)], [])

    def rope_tables(self):
        self.new_phase()
        ar = self.ar
        NT = self.NT
        self.cosT = self.dscr("cosT", [64, NT])
        self.sinT = self.dscr("sinT", [64, NT])
        invf = ar.alloc([1], F32)
        self.DMA('sp', invf[0:64, :], self.invf_in, [], ['invf'])
        MAGIC = 12582912.0
        C1 = 6.28125
        C2 = float(2 * np.pi - 6.28125)
        PI = float(np.pi)
        bufs = [[ar.alloc([T], F32) for _ in range(6)] for _ in range(2)]
        for t in range(self.NTILE):
            t0 = t * T
            pos, ang, kf, r, rc, o = [b[0:64, :] for b in bufs[t % 2]]
            R = lambda nm: '%s%d' % (nm, t % 2)
            self.DMA('sp', pos, self.pos_in[0:1, t0:t0 + T].partition_broadcast(64), [], [R('pos')])
            self.TS('dve', ang, pos, invf[0:64, 0:1], None, ALU.mult, None, [R('pos'), 'invf'], [R('ang')])
            self.TS('dve', kf, ang, float(1 / (2 * np.pi)), MAGIC, ALU.mult, ALU.add, [R('ang')], [R('kf')])
            self.TS('dve', kf, kf, MAGIC, None, ALU.subtract, None, [R('kf')], [R('kf')])
            self.STT('dve', r, kf, -C1, ang, ALU.mult, ALU.add, [R('kf'), R('ang')], [R('r')])
            self.STT('dve', r, kf, -C2, r, ALU.mult, ALU.add, [R('kf'), R('r')], [R('r')])
            self.TS('dve', rc, r, PI / 2, None, ALU.add, None, [R('r')], [R('rc')])
            self.TS('dve', kf, rc, PI, None, ALU.is_gt, None, [R('rc')], [R('kf')])
            self.STT('dve', rc, kf, -2 * PI, rc, ALU.mult, ALU.add, [R('kf'), R('rc')], [R('rc')])
            self.TS('dve', r, r, -PI, PI, ALU.max, ALU.min, [R('r')], [R('r')])
            self.TS('dve', rc, rc, -PI, PI, ALU.max, ALU.min, [R('rc')], [R('rc')])
            self.ACT(o, r, AF.Sin, [R('r')], [R('o')])
            self.DMA('sp', self.sinT[:, t0:t0 + T], o, [R('o')], [])
            self.ACT(pos, rc, AF.Sin, [R('rc')], [R('pos')])
            self.DMA('sp', self.cosT[:, t0:t0 + T], pos, [R('pos')], [])

    def mla_scratch(self):
        if hasattr(self, 'QnT'):
            return
        NT = self.NT
        d = self.dscr
        self.QnT = d("QnT", [8, 128, NT], BF16)
        self.QrT = d("QrT", [8, 64, NT], BF16)
        self.qnorm = d("qnorm", [8, NT], F32)
        self.KnT = d("KnT", [8, 128, NT], BF16)
        self.KrT = d("KrT", [65, NT], BF16)
        self.Vtok = d("Vtok", [NT, 1024], BF16)

    def o_scratch(self):
        if not hasattr(self, 'oT'):
            self.oT = self.dscr("oT", [8, 128, self.NT], BF16)

    def mla(self, jm, L):
        self.mla_scratch()
        self.o_scratch()
        stop = self.cfg.get('stop', '')
        if stop == 'rope':
            return
        self.mla_proj(jm, L)
        if stop == 'proj':
            return
        self.mla_attn(jm, L)
        if stop == 'attn':
            return
        self.pass_C(self.W['mla_w_out'][jm], L)

    def mla_proj(self, jm, L):
        self.new_phase()
        ar = self.ar
        W = self.W
        self.negkmax = ar.alloc([1], F32)
        self.mla_keep = ar.off
        kmax2 = ar.alloc([1], F32)
        self.epsc = ar.alloc([1], F32)
        eps_c = self.epsc
        self.MEMSET('dve', eps_c, EPS, ['epsc'])
        self.MEMSET('dve', kmax2, 0.0, ['kmax2'])
        win = ar.alloc([8, 448], BF16)
        winr = ar.alloc([8, 64], BF16)
        wuq = ar.alloc([2, 1536], BF16)
        wuqr = ar.alloc([2, 8, 64], BF16)
        wukv = ar.alloc([2048], BF16)
        wkn = ar.alloc([8, 128], BF16)
        wv = ar.alloc([8, 128], BF16)
        gq = ar.alloc([2], F32)
        gkv = ar.alloc([1], F32)
        self.stage_rows = ar.alloc([128], F32)
        onesrow = ar.alloc([T], BF16)
        self.DMA('pool', win, W['mla_w_in'][jm].rearrange("(k p) n -> p k n", p=128), [], ['win'])
        self.DMA('pool', wuq, W['mla_w_uq'][jm].rearrange("(k p) n -> p k n", p=128), [], ['wuq'])
        self.DMA('pool', wukv, W['mla_w_ukv'][jm], [], ['wukv'])
        self.p.dma('sp', gq, W['mla_g_q'][jm].rearrange("(c p) -> p c", p=128), [], ['gq'], allow_slow_non_contiguous=True)
        self.p.dma('sp', gkv, W['mla_g_kv'][jm].rearrange("(c p) -> p c", p=128), [], ['gkv'], allow_slow_non_contiguous=True)
        self.MEMSET('dve', onesrow[0:65, :], 1.0, ['onesrow'])
        wuq4 = wuq.rearrange("p k (h f) -> p k h f", f=192)
        for kc in range(2):
            self.TS('dve', wuqr[:, kc, :, 0:32], wuq4[:, kc, :, 160:192], -1.0, None, ALU.mult, None, ['wuq'], ['wuqr'])
            self.CP('dve', wuqr[:, kc, :, 32:64], wuq4[:, kc, :, 128:160], ['wuq'], ['wuqr'])
        self.TS('dve', winr[:, :, 0:32], win[:, :, 416:448], -1.0, None, ALU.mult, None, ['win'], ['winr'])
        self.CP('dve', winr[:, :, 32:64], win[:, :, 384:416], ['win'], ['winr'])
        wukv3 = wukv.rearrange("p (h f) -> p h f", f=256)
        self.CP('dve', wkn, wukv3[:, :, 0:128], ['wukv'], ['wkn'])
        self.CP('dve', wv, wukv3[:, :, 128:256], ['wukv'], ['wv'])
        wvf = wv.rearrange("p h f -> p (h f)")

        xt = [ar.alloc([8, T], F32) for _ in range(2)]
        sq = ar.alloc([8, T], BF16)
        h = ar.alloc([8, T], BF16)
        rstd = ar.alloc([T], F32)
        tmp = [ar.alloc([T], F32) for _ in range(2)]
        cq = ar.alloc([2, T], F32)
        ckv = ar.alloc([T], F32)
        cqn = ar.alloc([2, T], BF16)
        ckvn = ar.alloc([T], BF16)
        rs2 = ar.alloc([T], F32)
        cos = [ar.alloc([T], F32) for _ in range(2)]
        sin = [ar.alloc([T], F32) for _ in range(2)]
        stb = [ar.alloc([T], BF16) for _ in range(4)]
        sqa = [ar.alloc([T], BF16) for _ in range(2)]
        sqb = [ar.alloc([T], BF16) for _ in range(2)]
        t1 = [ar.alloc([T], F32) for _ in range(2)]
        t2 = [ar.alloc([T], F32) for _ in range(2)]
        qnt = [ar.alloc([T], F32) for _ in range(2)]
        rmx = ar.alloc([1], F32)
        vst = [ar.alloc([1024], BF16) for _ in range(2)]
        xTv = self.xT.rearrange("c p t -> p c t")
        PS = self.ps
        cnt = {'st': 0, 'sq': 0, 't': 0, 'q': 0, 'v': 0, 'pa': 0, 'pb': 0}

        def nxt(k, n):
            v = cnt[k] % n
            cnt[k] += 1
            return v

        def load(t):
            self.DMA('sp', xt[t % 2], xTv[:, :, t * T:(t + 1) * T], [], [('xt%d' % (t % 2), c) for c in range(8)])
            self.DMA('sp', cos[t % 2][0:64, :], self.cosT[:, t * T:(t + 1) * T], [], ['cos%d' % (t % 2)])
            self.DMA('sp', sin[t % 2][0:64, :], self.sinT[:, t * T:(t + 1) * T], [], ['sin%d' % (t % 2)])

        def rope_out(p_x, p_xr, xres, xrres, cs, sn, csr, snr, dst_dram):
            i = nxt('t', 2)
            a, b = t1[i][0:64, :], t2[i][0:64, :]
            self.TT('dve', a, p_x, cs, ALU.mult, [xres, csr], ['t1_%d' % i])
            self.TT('dve', b, p_xr, sn, ALU.mult, [xrres, snr], ['t2_%d' % i])
            si = nxt('st', 4)
            self.TT('pool', stb[si][0:64, :], a, b, ALU.add, ['t1_%d' % i, 't2_%d' % i], ['stb%d' % si])
            self.DMA('sp', dst_dram, stb[si][0:64, :], ['stb%d' % si], [])
            return si

        lvl = int(self.cfg.get('lvl', 9))
        load(0)
        for t in range(self.NTILE):
            t0 = t * T
            s = t0 // self.SEG
            if t + 1 < self.NTILE:
                load(t + 1)
            x = xt[t % 2]
            xr = 'xt%d' % (t % 2)
            cs, sn = cos[t % 2][0:64, :], sin[t % 2][0:64, :]
            csr, snr = 'cos%d' % (t % 2), 'sin%d' % (t % 2)
            self.norm_mod(x, lambda c: (xr, c), sq, lambda c: ('sq', c), PS[6], 'ps6', rstd, tmp, h,
                          lambda c: ('h', c), lambda c: self.Acol(L, 1, c, s), lambda c: self.Bcol(L, 1, c, s))
            hres = [('h', k) for k in range(8)]
            for m in range(2):
                for k in range(8):
                    self.MM(PS[m], win[:, k, m * 128:(m + 1) * 128], h[:, k, :], k == 0, k == 7, ['win', ('h', k)], ['ps%d' % m])
                self.CP('act', cq[:, m, :], PS[m], ['ps%d' % m], [('cq', m)])
                self.ACT(sq[:, m, :], PS[m], AF.Square, ['ps%d' % m], [('sq', m)])
            for m in range(2):
                self.MM(PS[6], self.ones_bf, sq[:, m, :], m == 0, m == 1, ['ones', ('sq', m)], ['ps6'])
            self.ACT(rs2, PS[6], AF.Sqrt, ['ps6', 'epsc'], ['rs2'], bias=eps_c, scale=1.0 / 256)
            self.p.op('dve', lambda e: e.reciprocal(out=rs2, in_=rs2), ['rs2'], ['rs2'])
            for m in range(2):
                self.STT('dve', cqn[:, m, :], cq[:, m, :], gq[:, m:m + 1], rs2, ALU.mult, ALU.mult,
                         [('cq', m), 'gq', 'rs2'], [('cqn', m)])
            for k in range(8):
                self.MM(PS[2], win[:, k, 256:384], h[:, k, :], k == 0, k == 7, ['win', ('h', k)], ['ps2'])
            self.CP('act', ckv, PS[2], ['ps2'], ['ckv'])
            self.ACT(sq[:, 2, :], PS[2], AF.Square, ['ps2'], [('sq', 2)])
            self.MM(PS[6], self.ones_bf, sq[:, 2, :], True, True, ['ones', ('sq', 2)], ['ps6'])
            self.ACT(rs2, PS[6], AF.Sqrt, ['ps6', 'epsc'], ['rs2'], bias=eps_c, scale=1.0 / 128)
            self.p.op('dve', lambda e: e.reciprocal(out=rs2, in_=rs2), ['rs2'], ['rs2'])
            self.STT('dve', ckvn, ckv, gkv[:, 0:1], rs2, ALU.mult, ALU.mult, ['ckv', 'gkv', 'rs2'], ['ckvn'])
            if lvl < 1:
                continue
            for k in range(8):
                self.MM(PS[3][0:64, :], win[:, k, 384:448], h[:, k, :], k == 0, k == 7, ['win', ('h', k)], ['ps3'])
            for k in range(8):
                self.MM(PS[4][0:64, :], winr[:, k, :], h[:, k, :], k == 0, k == 7, ['winr', ('h', k)], ['ps4'])
            i = nxt('sq', 2)
            sr = rope_out(PS[3][0:64, :], PS[4][0:64, :], 'ps3', 'ps4', cs, sn, csr, snr, self.KrT[0:64, t0:t0 + T])
            self.ACT(sqb[i][0:64, :], stb[sr][0:64, :], AF.Square, ['stb%d' % sr], ['sqb%d' % i])
            krsq, krsq_res = sqb[i][0:64, :], 'sqb%d' % i
            self.DMA('sp', self.KrT[64:65, t0:t0 + T], onesrow[64:65, :], ['onesrow'], [])
            if lvl < 2:
                continue
            for hh in range(8):
                b = nxt('pa', 2)
                self.MM(PS[b], wkn[:, hh, :], ckvn, True, True, ['wkn', 'ckvn'], ['ps%d' % b])
                si = nxt('st', 4)
                self.CP('act', stb[si], PS[b], ['ps%d' % b], ['stb%d' % si])
                self.DMA('sp', self.KnT[hh][:, t0:t0 + T], stb[si], ['stb%d' % si], [])
                qi = nxt('sq', 2)
                self.ACT(sqa[qi], PS[b], AF.Square, ['ps%d' % b], ['sqa%d' % qi])
                self.MM(PS[7], self.ones_bf, sqa[qi], True, False, ['ones', 'sqa%d' % qi], ['ps7'])
                self.MM(PS[7], self.ones_bf[0:64, :], krsq, False, True, ['ones', krsq_res], ['ps7'])
                self.p.op('dve', lambda e: e.reduce_max(out=rmx, in_=PS[7], axis=AX.X), ['ps7'], ['rmx'])
                self.TT('dve', kmax2, kmax2, rmx, ALU.max, ['kmax2', 'rmx'], ['kmax2'])
            if lvl < 3:
                continue
            for i4 in range(4):
                vi = nxt('v', 2)
                for half in range(2):
                    b = nxt('pa', 2)
                    self.MM(PS[b], ckvn[:, i4 * 128:(i4 + 1) * 128], wvf[:, half * 512:(half + 1) * 512], True, True,
                            ['ckvn', 'wv'], ['ps%d' % b])
                    self.CP('act' if half == 0 else 'dve', vst[vi][:, half * 512:(half + 1) * 512], PS[b],
                            ['ps%d' % b], [('vst%d' % vi, half)])
                self.DMA('sp', self.Vtok[t0 + i4 * 128:t0 + (i4 + 1) * 128, :], vst[vi],
                         [('vst%d' % vi, 0), ('vst%d' % vi, 1)], [])
            if lvl < 4:
                continue
            for hh in range(8):
                b = nxt('pa', 2)
                for kc in range(2):
                    self.MM(PS[b], wuq[:, kc, hh * 192:hh * 192 + 128], cqn[:, kc, :], kc == 0, kc == 1,
                            ['wuq', ('cqn', kc)], ['ps%d' % b])
                b2 = nxt('pb', 2)
                pr, prr = PS[2 + b2][0:64, :], PS[4 + b2][0:64, :]
                for kc in range(2):
                    self.MM(pr, wuq[:, kc, hh * 192 + 128:hh * 192 + 192], cqn[:, kc, :], kc == 0, kc == 1,
                            ['wuq', ('cqn', kc)], ['ps%d' % (2 + b2)])
                for kc in range(2):
                    self.MM(prr, wuqr[:, kc, hh, :], cqn[:, kc, :], kc == 0, kc == 1,
                            ['wuqr', ('cqn', kc)], ['ps%d' % (4 + b2)])
                si = nxt('st', 4)
                self.CP('act', stb[si], PS[b], ['ps%d' % b], ['stb%d' % si])
                self.DMA('sp', self.QnT[hh][:, t0:t0 + T], stb[si], ['stb%d' % si], [])
                qi = nxt('sq', 2)
                self.ACT(sqa[qi], PS[b], AF.Square, ['ps%d' % b], ['sqa%d' % qi])
                sr = rope_out(pr, prr, 'ps%d' % (2 + b2), 'ps%d' % (4 + b2), cs, sn, csr, snr, self.QrT[hh][:, t0:t0 + T])
                self.ACT(sqb[qi][0:64, :], stb[sr][0:64, :], AF.Square, ['stb%d' % sr], ['sqb%d' % qi])
                self.MM(PS[7], self.ones_bf, sqa[qi], True, False, ['ones', 'sqa%d' % qi], ['ps7'])
                self.MM(PS[7], self.ones_bf[0:64, :], sqb[qi][0:64, :], False, True, ['ones', 'sqb%d' % qi], ['ps7'])
                ni = nxt('q', 2)
                self.ACT(qnt[ni][0:1, :], PS[7][0:1, :], AF.Sqrt, ['ps7'], ['qnt%d' % ni])
                self.DMA('sp', self.qnorm[hh:hh + 1, t0:t0 + T], qnt[ni][0:1, :], ['qnt%d' % ni], [])
        self.ACT(self.negkmax, kmax2, AF.Sqrt, ['kmax2'], ['negkmax'])
        self.TS('dve', self.negkmax, self.negkmax, -1.0, None, ALU.mult, None, ['negkmax'], ['negkmax'])

    def mla_attn(self, jm, L):
        self.p.barrier()
        ar = self.ar
        ar.off = self.mla_keep
        SEG = self.SEG
        zcol = ar.alloc([1], F32)
        self.MEMSET('dve', zcol, 0.0, ['zcol'])
        ones32 = ar.alloc([128], F32)
        self.MEMSET('dve', ones32, 1.0, ['ones32'])
        crossb = self.flags[:, 1:2]
        NKmax = 2 * SEG
        kr = [ar.alloc([NKmax], BF16) for _ in range(2)]
        kn = [ar.alloc([NKmax], BF16) for _ in range(2)]
        vv = [ar.alloc([NKmax // 128, 128], BF16) for _ in range(2)]
        qn = [ar.alloc([T], BF16) for _ in range(2)]
        qr = [ar.alloc([T], BF16) for _ in range(2)]
        qf = [ar.alloc([T], F32) for _ in range(2)]
        pT2 = [ar.alloc([2 * T], BF16) for _ in range(3)]
        acc = [ar.alloc([2 * T], F32) for _ in range(2)]
        rec = [ar.alloc([T], F32) for _ in range(2)]
        ob = [ar.alloc([T], BF16) for _ in range(2)]
        PS = self.ps
        scale = float(192 ** -0.5)
        Vv = self.Vtok.rearrange("(n p) f -> p n f", p=128)
        groups = [(0, 2 * SEG), (2 * SEG, SEG)]
        heads = [(g, hh) for g in range(2) for hh in range(8)]
        items = [(hi, tq) for hi, (g, hh) in enumerate(heads) for tq in range(groups[g][1] // T)]

        def load_head(hi):
            g, hh = heads[hi]
            k0, NK = groups[g]
            hb = hi % 2
            if hh == 0:
                self.DMA('sp', kr[g][0:65, 0:NK], self.KrT[:, k0:k0 + NK], [], ['kr%d' % g])
            self.DMA('sp', kn[hb][:, 0:NK], self.KnT[hh][:, k0:k0 + NK], [], ['kn%d' % hb])
            self.DMA('sp', vv[hb][:, 0:NK // 128, :], Vv[:, k0 // 128:(k0 + NK) // 128, hh * 128:(hh + 1) * 128], [],
                     ['vv%d' % hb])

        def load_q(ii):
            hi, tq = items[ii]
            g, hh = heads[hi]
            q0 = groups[g][0] + tq * T
            qb = ii % 2
            self.DMA('sp', qn[qb], self.QnT[hh][:, q0:q0 + T], [], ['qn%d' % qb])
            self.DMA('sp', qr[qb][0:64, :], self.QrT[hh][:, q0:q0 + T], [], [('qr%d' % qb, 0)])
            self.DMA('sp', qf[qb][64:65, :], self.qnorm[hh:hh + 1, q0:q0 + T], [], ['qf%d' % qb])

        npt = 0
        load_head(0)
        load_q(0)
        for ii, (hi, tq) in enumerate(items):
            g, hh = heads[hi]
            k0, NK = groups[g]
            nsub = NK // 128
            npairs = nsub // 2
            hb = hi % 2
            qb = ii % 2
            q0 = k0 + tq * T
            sq_seg = q0 // SEG
            if tq == 0 and hi + 1 < len(heads):
                load_head(hi + 1)
            if ii + 1 < len(items):
                load_q(ii + 1)
            self.TS('dve', qr[qb][64:65, :], qf[qb][64:65, :], self.negkmax[64:65, 0:1], None, ALU.mult, None,
                    ['qf%d' % qb, 'negkmax'], [('qr%d' % qb, 1)])
            qrres = [('qr%d' % qb, 0), ('qr%d' % qb, 1)]
            Ops, Ores = PS[4 + qb], 'ps%d' % (4 + qb)
            Sps, Sres = PS[6 + qb], 'ps%d' % (6 + qb)

            def scores(pr):
                for j in range(2):
                    kk = 2 * pr + j
                    b = 2 * (pr % 2) + j
                    self.MM(PS[b], kn[hb][:, kk * 128:(kk + 1) * 128], qn[qb], True, False,
                            ['kn%d' % hb, 'qn%d' % qb], ['ps%d' % b])
                    self.MM(PS[b], kr[g][0:65, kk * 128:(kk + 1) * 128], qr[qb][0:65, :], False, True,
                            ['kr%d' % g] + qrres, ['ps%d' % b])

            scores(0)
            for pr in range(npairs):
                if pr + 1 < npairs:
                    scores(pr + 1)
                pi = npt % 3
                npt += 1
                kseg = (k0 + 2 * pr * 128) // SEG
                bias = zcol if kseg == sq_seg else crossb
                b0 = 2 * (pr % 2)
                self.ACT(pT2[pi], self.pp[pr % 2], AF.Exp, ['ps%d' % b0, 'ps%d' % (b0 + 1), 'zcol', 'flags'],
                         ['pT%d' % pi], bias=bias, scale=scale)
                for j in range(2):
                    self.MM(Ops, vv[hb][:, 2 * pr + j, :], pT2[pi][:, j * T:(j + 1) * T], pr == 0 and j == 0,
                            pr == npairs - 1 and j == 1, ['vv%d' % hb, 'pT%d' % pi], [Ores])
                if pr == 0:
                    self.CP('dve', acc[qb], pT2[pi], ['pT%d' % pi], ['acc%d' % qb])
                else:
                    self.TT('dve', acc[qb], acc[qb], pT2[pi], ALU.add, ['acc%d' % qb, 'pT%d' % pi], ['acc%d' % qb])
            self.MM(Sps, ones32, acc[qb][:, 0:T], True, False, ['ones32', 'acc%d' % qb], [Sres])
            self.MM(Sps, ones32, acc[qb][:, T:2 * T], False, True, ['ones32', 'acc%d' % qb], [Sres])
            self.p.op('dve', (lambda r_, s_: (lambda e: e.reciprocal(out=r_, in_=s_)))(rec[qb], Sps),
                      [Sres], ['rec%d' % qb])
            self.TT('dve', ob[qb], Ops, rec[qb], ALU.mult, [Ores, 'rec%d' % qb], ['ob%d' % qb])
            self.DMA('sp', self.oT[hh][:, q0:q0 + T], ob[qb], ['ob%d' % qb], [])

    def pass_C(self, w_out, L):
        self.new_phase()
        ar = self.ar
        wo_off = ar.off
        ar.off = wo_off + 3 * 8 * DFF * 2
        self.ffn_w_end = ar.off
        wo = ar.alloc([8, D], BF16)
        xt = [ar.alloc([8, T], F32) for _ in range(2)]
        ot = [ar.alloc([8, T], BF16) for _ in range(2)]
        wov = w_out.rearrange("(k p) n -> p k n", p=128)
        for k in range(0, 8, 2):
            self.DMA('pool', wo[:, k:k + 2, :], wov[:, k:k + 2, :], [], [('wo', k), ('wo', k + 1)])
        self.ffn_pre = None
        xTv = self.xT.rearrange("c p t -> p c t")
        oTv = self.oT.rearrange("c p t -> p c t")

        def load(t):
            self.DMA('sp', xt[t % 2], xTv[:, :, t * T:(t + 1) * T], [], [('xt%d' % (t % 2), c) for c in range(8)])
            self.DMA('sp', ot[t % 2], oTv[:, :, t * T:(t + 1) * T], [], ['ot%d' % (t % 2)])

        load(0)
        nd = 0
        for t in range(self.NTILE):
            t0 = t * T
            s = t0 // self.SEG
            if t + 1 < self.NTILE:
                load(t + 1)
            x = xt[t % 2]
            xr = 'xt%d' % (t % 2)
            o = ot[t % 2]
            for mo in range(8):
                b = nd % 2
                nd += 1
                for c in range(8):
                    self.MM(self.ps[b], wo[:, c, mo * 128:(mo + 1) * 128], o[:, c, :], c == 0, c == 7,
                            [('wo', c), 'ot%d' % (t % 2)], ['ps%d' % b])
                self.STT('dve', x[:, mo, :], self.ps[b], self.Gcol(L, 1, mo, s), x[:, mo, :], ALU.mult, ALU.add,
                         ['ps%d' % b, 'Gar', (xr, mo)], [(xr, mo)])
            self.DMA('sp', xTv[:, :, t0:t0 + T], x, [(xr, c) for c in range(8)], [])


    def gla_scratch(self):
        if hasattr(self, 'QinT'):
            return
        NT = self.NT
        d = self.dscr
        self.QinT = [d("QinT%d" % i, [4, 128, NT], BF16) for i in range(2)]
        self.KinT = [d("KinT%d" % i, [4, 128, NT], BF16) for i in range(2)]
        self.Kout = [d("Kout%d" % i, [NT, 512], BF16) for i in range(2)]
        self.Vg = d("Vg", [NT, 1024], BF16)
        self.rT = d("rT", [8, 128, NT], BF16)
        self.obT = d("obT", [8, 128, NT], BF16)

    def gla(self, jg, L):
        self.gla_scratch()
        self.o_scratch()
        self.gla_proj(jg, L)
        self.gla_scan(jg, L)
        self.pass_C(self.W['gla_w_out'][jg], L)

    def gla_proj(self, jg, L):
        self.new_phase()
        ar = self.ar
        W = self.W
        NSUB = self.NT // 128
        self.dec = ar.alloc([2, 4, NSUB], F32)
        self.gla_keep = ar.off
        self.epsc = ar.alloc([1], F32)
        self.MEMSET('dve', self.epsc, EPS, ['epsc'])
        win = ar.alloc([8, 3104], BF16)
        wup = ar.alloc([2, 512], BF16)
        tri = ar.alloc([4, 128], BF16)
        winv = W['gla_w_in'][jg].rearrange("(k p) n -> p k n", p=128)
        for k in range(8):
            self.DMA('pool', win[:, k, :], winv[:, k, :], [], [('win', k)])
        winres = [('win', k) for k in range(8)]
        self.DMA('pool', wup[0:16, :, :], W['gla_w_gate_up'][jg].rearrange("d r n -> r d n"), [], [('wup', 0)])
        self.DMA('pool', wup[16:17, :, :], W['gla_b_gate'][jg:jg + 1], [], [('wup', 1)])
        self.DMA('pool', tri, self.tri_in.rearrange("a p n -> p a n"), [], ['tri'])
        wupres = [('wup', 0), ('wup', 1)]
        xt = [ar.alloc([8, T], F32) for _ in range(2)]
        sq = ar.alloc([8, T], BF16)
        h = ar.alloc([8, T], BF16)
        rstd = ar.alloc([T], F32)
        tmp = [ar.alloc([T], F32) for _ in range(2)]
        qk = ar.alloc([8, T], F32)
        aT = [ar.alloc([T], BF16) for _ in range(2)]
        e32 = [ar.alloc([T], F32) for _ in range(2)]
        labf = [ar.alloc([T], BF16) for _ in range(2)]
        E = [ar.alloc([4, 128], F32) for _ in range(2)]
        Ei = [ar.alloc([4, 128], F32) for _ in range(2)]
        Eo = [ar.alloc([T], F32) for _ in range(2)]
        rst = [ar.alloc([T], BF16) for _ in range(2)]
        kst = [ar.alloc([T], BF16) for _ in range(2)]
        vst = [ar.alloc([1024], BF16) for _ in range(2)]
        qin_st = [ar.alloc([4, T], BF16) for _ in range(2)]
        kin_st = [ar.alloc([4, T], BF16) for _ in range(2)]
        for d_ in range(2):
            self.MEMSET('dve', aT[d_][0:17, :], 1.0, ['aT%d' % d_])
        xTv = self.xT.rearrange("c p t -> p c t")
        PS = self.ps
        cnt = {}

        def nxt(k, n):
            v = cnt.get(k, 0) % n
            cnt[k] = cnt.get(k, 0) + 1
            return v

        def load(t):
            self.DMA('sp', xt[t % 2], xTv[:, :, t * T:(t + 1) * T], [], [('xt%d' % (t % 2), c) for c in range(8)])

        load(0)
        for t in range(self.NTILE):
            t0 = t * T
            s = t0 // self.SEG
            if t + 1 < self.NTILE:
                load(t + 1)
            x = xt[t % 2]
            xr = 'xt%d' % (t % 2)
            self.norm_mod(x, lambda c: (xr, c), sq, lambda c: ('sq', c), PS[7], 'ps7', rstd, tmp, h,
                          lambda c: ('h', c), lambda c: self.Acol(L, 1, c, s), lambda c: self.Bcol(L, 1, c, s))
            for m in range(8):
                b = nxt('pa', 2)
                for k in range(8):
                    self.MM(PS[b], win[:, k, m * 128:(m + 1) * 128], h[:, k, :], k == 0, k == 7,
                            [('win', k), ('h', k)], ['ps%d' % b])
                self.ACT(qk[:, m, :], PS[b], AF.Identity, ['ps%d' % b], [('qk', m)],
                         scale=(float(128 ** -0.5) if m < 4 else 1.0))
            for m in range(8):
                b = nxt('pa', 2)
                for k in range(8):
                    self.MM(PS[b], win[:, k, 2048 + m * 128:2048 + (m + 1) * 128], h[:, k, :], k == 0, k == 7,
                            [('win', k), ('h', k)], ['ps%d' % b])
                ri = nxt('r', 2)
                self.ACT(rst[ri], PS[b], AF.Silu, ['ps%d' % b], ['rst%d' % ri])
                self.DMA('sp', self.rT[m][:, t0:t0 + T], rst[ri], ['rst%d' % ri], [])
            for d_ in range(2):
                for k in range(8):
                    self.MM(PS[2][0:16, :], win[:, k, 3072 + d_ * 16:3088 + d_ * 16], h[:, k, :], k == 0, k == 7,
                            [('win', k), ('h', k)], ['ps2'])
                self.CP('act', aT[d_][0:16, :], PS[2][0:16, :], ['ps2'], ['aT%d' % d_])
            for i4 in range(4):
                sub = slice(i4 * 128, (i4 + 1) * 128)
                n = t * 4 + i4
                for d_ in range(2):
                    self.MM(PS[2 + d_], aT[d_][0:17, sub], wup[0:17, d_, :], True, True, ['aT%d' % d_] + wupres,
                            ['ps%d' % (2 + d_)])
                    self.ACT(e32[d_], PS[2 + d_], AF.Exp, ['ps%d' % (2 + d_)], ['e32_%d' % d_], scale=-1.0)
                    self.ACT(e32[d_], e32[d_], AF.Ln, ['e32_%d' % d_], ['e32_%d' % d_], bias=1.0)
                    self.TS('dve', labf[d_], e32[d_], -1.0 / 16.0, None, ALU.mult, None, ['e32_%d' % d_], ['la%d' % d_])
                for k in range(8):
                    self.MM(PS[4], h[:, k, sub], win[:, k, 512:1024], k == 0, k == 7, [('h', k), ('win', k)], ['ps4'])
                for d_ in range(2):
                    self.MM(PS[5 + d_], tri[:, 2 + d_, :], labf[d_], True, True, ['tri', 'la%d' % d_], ['ps%d' % (5 + d_)])
                    self.ACT(Eo[d_], PS[5 + d_], AF.Exp, ['ps%d' % (5 + d_)], ['Eo%d' % d_])
                    ki = nxt('k', 2)
                    self.TT('dve', kst[ki], PS[4], Eo[d_], ALU.mult, ['ps4', 'Eo%d' % d_], ['kst%d' % ki])
                    self.DMA('sp', self.Kout[d_][t0 + i4 * 128:t0 + (i4 + 1) * 128, :], kst[ki], ['kst%d' % ki], [])
                vi = nxt('v', 2)
                for half in range(2):
                    b = nxt('pa', 2)
                    for k in range(8):
                        self.MM(PS[b], h[:, k, sub], win[:, k, 1024 + half * 512:1024 + (half + 1) * 512], k == 0, k == 7,
                                [('h', k), ('win', k)], ['ps%d' % b])
                    self.CP('act', vst[vi][:, half * 512:(half + 1) * 512], PS[b], ['ps%d' % b], [('vst%d' % vi, half)])
                self.DMA('sp', self.Vg[t0 + i4 * 128:t0 + (i4 + 1) * 128, :], vst[vi],
                         [('vst%d' % vi, 0), ('vst%d' % vi, 1)], [])
                for d_ in range(2):
                    for hh in range(4):
                        self.MM(PS[2 + d_][:, hh * 128:(hh + 1) * 128], labf[d_][:, hh * 128:(hh + 1) * 128], tri[:, d_, :],
                                True, True, ['la%d' % d_, 'tri'], ['ps%d' % (2 + d_)])
                    pv = PS[2 + d_].rearrange("p (a b) -> p a b", b=128)
                    self.ACT(E[d_], pv, AF.Exp, ['ps%d' % (2 + d_)], ['E%d' % d_])
                    self.ACT(Ei[d_], pv, AF.Exp, ['ps%d' % (2 + d_)], ['Ei%d' % d_], scale=-1.0)
                    self.TT('dve', qin_st[d_][:, :, sub], qk[:, 0:4, sub], E[d_], ALU.mult,
                            [('qk', m) for m in range(4)] + ['E%d' % d_], [('qin%d' % d_, i4)])
                    self.TT('dve', kin_st[d_][:, :, sub], qk[:, 4:8, sub], Ei[d_], ALU.mult,
                            [('qk', m) for m in range(4, 8)] + ['Ei%d' % d_], [('kin%d' % d_, i4)])
                    col = 127 if d_ == 0 else 0
                    self.CP('dve', self.dec[:, d_, :, n], E[d_][:, :, col], ['E%d' % d_], ['dec'])
            for d_ in range(2):
                self.DMA('sp', self.QinT[d_].rearrange("c p t -> p c t")[:, :, t0:t0 + T], qin_st[d_],
                         [('qin%d' % d_, i) for i in range(4)], [])
                self.DMA('sp', self.KinT[d_].rearrange("c p t -> p c t")[:, :, t0:t0 + T], kin_st[d_],
                         [('kin%d' % d_, i) for i in range(4)], [])

    def gla_scan(self, jg, L):
        self.p.barrier()
        ar = self.ar
        ar.off = self.gla_keep
        W = self.W
        SEG = self.SEG
        NSUB = self.NT // 128
        SPS = SEG // 128
        self.epsc = ar.alloc([1], F32)
        self.MEMSET('dve', self.epsc, EPS, ['epsc'])
        gn = ar.alloc([2], F32)
        self.p.dma('sp', gn, W['gla_g_norm'][jg].rearrange("(c p) -> p c", p=128), [], ['gn'],
                   allow_slow_non_contiguous=True)
        mask1 = ar.alloc([128], F32)
        maskf = ar.alloc([4, 128], F32)
        S = ar.alloc([4, 256], F32)
        Sbf = ar.alloc([4, 256], BF16)
        qin = [ar.alloc([4, T], BF16) for _ in range(2)]
        kin = [ar.alloc([4, T], BF16) for _ in range(2)]
        kout = [ar.alloc([4, 512], BF16) for _ in range(2)]
        vt = [ar.alloc([4, 1024], BF16) for _ in range(2)]
        obt = [ar.alloc([8, T], BF16) for _ in range(2)]
        rt = [ar.alloc([8, T], BF16) for _ in range(2)]
        ost = [ar.alloc([8, T], BF16) for _ in range(2)]
        o32 = ar.alloc([8, T], F32)
        sq = ar.alloc([8, T], BF16)
        am = [ar.alloc([4, 128], BF16) for _ in range(2)]
        rs = ar.alloc([T], F32)
        tm = [ar.alloc([T], F32) for _ in range(2)]
        PS = self.ps
        link = self.flags[:, 0:1]
        Kv = [self.Kout[d_].rearrange("(n p) f -> p n f", p=128) for d_ in range(2)]
        Vv = self.Vg.rearrange("(n p) f -> p n f", p=128)
        Qv = [self.QinT[d_].rearrange("c p t -> p c t") for d_ in range(2)]
        Knv = [self.KinT[d_].rearrange("c p t -> p c t") for d_ in range(2)]
        obv = self.obT.rearrange("c p t -> p c t")
        rv = self.rT.rearrange("c p t -> p c t")
        ov = self.oT.rearrange("c p t -> p c t")
        na = 0
        for d_ in (1, 0):
            if d_ == 0:
                self.p.barrier()
            self.DMA('sp', mask1, self.tri_in[d_], [], ['mask1'])
            for hh in range(4):
                self.CP('dve', maskf[:, hh, :], mask1, ['mask1'], ['maskf'])
            tiles = list(range(self.NTILE))
            if d_ == 1:
                tiles = tiles[::-1]

            def load(t):
                bi = t % 2
                self.DMA('sp', qin[bi], Qv[d_][:, :, t * T:(t + 1) * T], [], ['qin%d' % bi])
                self.DMA('sp', kin[bi], Knv[d_][:, :, t * T:(t + 1) * T], [], ['kin%d' % bi])
                self.DMA('sp', kout[bi], Kv[d_][:, t * 4:(t + 1) * 4, :], [], ['kout%d' % bi])
                self.DMA('sp', vt[bi], Vv[:, t * 4:(t + 1) * 4, :], [], ['vt%d' % bi])
                if d_ == 0:
                    self.DMA('sp', obt[bi], obv[:, :, t * T:(t + 1) * T], [], ['obt%d' % bi])
                    self.DMA('sp', rt[bi], rv[:, :, t * T:(t + 1) * T], [], ['rt%d' % bi])

            load(tiles[0])
            for ti, t in enumerate(tiles):
                t0 = t * T
                bi = t % 2
                if ti + 1 < len(tiles):
                    load(tiles[ti + 1])
                subs = range(4) if d_ == 0 else range(3, -1, -1)
                for i4 in subs:
                    n = t * 4 + i4
                    sub = slice(i4 * 128, (i4 + 1) * 128)
                    seg_i = n // SPS
                    first = (n % SPS == 0) if d_ == 0 else (n % SPS == SPS - 1)
                    if first:
                        linked = (seg_i == 1) if d_ == 0 else (seg_i == 0)
                        if linked:
                            self.TS('dve', S, S, link, None, ALU.mult, None, ['S', 'flags'], ['S'])
                            self.CP('act', Sbf, S, ['S'], ['Sbf'])
                        else:
                            self.MEMSET('dve', S, 0.0, ['S'])
                            self.MEMSET('dve', Sbf, 0.0, ['Sbf'])
                    ab = na % 2
                    na += 1
                    for hh in range(4):
                        self.MM(PS[ab][:, hh * 128:(hh + 1) * 128], kin[bi][:, hh, sub], qin[bi][:, hh, sub], True, True,
                                ['kin%d' % bi, 'qin%d' % bi], ['ps%d' % ab])
                    self.TT('dve', am[ab], PS[ab].rearrange("p (a b) -> p a b", b=128), maskf, ALU.mult,
                            ['ps%d' % ab, 'maskf'], ['am%d' % ab])
                    for hh in range(4):
                        for vc in range(2):
                            c = hh * 2 + vc
                            bank = 2 + c // 4
                            oo = PS[bank][:, (c % 4) * 128:(c % 4 + 1) * 128]
                            self.MM(oo, vt[bi][:, i4, c * 128:(c + 1) * 128], am[ab][:, hh, :], True, False,
                                    ['vt%d' % bi, 'am%d' % ab], ['ps%d' % bank])
                            self.MM(oo, Sbf[:, hh, vc * 128:(vc + 1) * 128], qin[bi][:, hh, sub], False, True,
                                    ['Sbf', 'qin%d' % bi], ['ps%d' % bank])
                    for hh in range(4):
                        bank = 4 + hh // 2
                        self.MM(PS[bank][:, (hh % 2) * 256:(hh % 2 + 1) * 256], kout[bi][:, i4, hh * 128:(hh + 1) * 128],
                                vt[bi][:, i4, hh * 256:(hh + 1) * 256], True, True, ['kout%d' % bi, 'vt%d' % bi],
                                ['ps%d' % bank])
                    for half in range(2):
                        pv = PS[2 + half].rearrange("p (a b) -> p a b", b=128)
                        if d_ == 1:
                            self.CP('act', ost[bi][:, half * 4:(half + 1) * 4, sub], pv, ['ps%d' % (2 + half)],
                                    [('ost%d' % bi, i4, half)])
                        else:
                            self.TT('dve', o32[:, half * 4:(half + 1) * 4, sub], pv, obt[bi][:, half * 4:(half + 1) * 4, sub],
                                    ALU.add, ['ps%d' % (2 + half), 'obt%d' % bi], [('o32', i4, half)])
                    for hh in range(4):
                        bank = 4 + hh // 2
                        self.STT('dve', S[:, hh, :], S[:, hh, :], self.dec[:, d_, hh, n:n + 1],
                                 PS[bank][:, (hh % 2) * 256:(hh % 2 + 1) * 256], ALU.mult, ALU.add,
                                 ['S', 'dec', 'ps%d' % bank], ['S'])
                    self.CP('act', Sbf, S, ['S'], ['Sbf'])
                if d_ == 1:
                    self.DMA('sp', obv[:, :, t0:t0 + T], ost[bi],
                             [('ost%d' % bi, i, hf) for i in range(4) for hf in range(2)], [])
                else:
                    ores = [('o32', i, hf) for i in range(4) for hf in range(2)]
                    for c in range(8):
                        self.ACT(sq[:, c, :], o32[:, c, :], AF.Square, ores, [('sq', c)])
                    for hh in range(4):
                        pb = 6 + hh % 2
                        for vc in range(2):
                            self.MM(PS[pb], self.ones_bf, sq[:, hh * 2 + vc, :], vc == 0, vc == 1,
                                    ['ones', ('sq', hh * 2 + vc)], ['ps%d' % pb])
                        self.ACT(rs, PS[pb], AF.Sqrt, ['ps%d' % pb, 'epsc'], ['rs'], bias=self.epsc, scale=1.0 / 256)
                        self.p.op('dve', lambda e: e.reciprocal(out=rs, in_=rs), ['rs'], ['rs'])
                        for vc in range(2):
                            c = hh * 2 + vc
                            self.STT('dve', tm[vc], o32[:, c, :], gn[:, vc:vc + 1], rs, ALU.mult, ALU.mult,
                                     ores + ['gn', 'rs'], ['tm%d' % vc])
                            self.TT('pool', ost[bi][:, c, :], tm[vc], rt[bi][:, c, :], ALU.mult,
                                    ['tm%d' % vc, 'rt%d' % bi], [('ostc%d' % bi, c)])
                    self.DMA('sp', ov[:, :, t0:t0 + T], ost[bi], [('ostc%d' % bi, c) for c in range(8)], [])

    def build(self):
        self.prologue()
        self.mixsel = self.cfg.get('mixsel', 1)
        if self.mixers and self.mixsel in (1, 3):
            self.rope_tables()
        self.pass_T()
        for L in range(self.depth):
            self.pass_F(L, 0)
            if self.mixers:
                if L % 2 == 0:
                    if self.mixsel in (1, 2):
                        self.gla(L // 2, L)
                else:
                    if self.mixsel in (1, 3):
                        self.mla(L // 2, L)
            pre = getattr(self, 'ffn_pre', None)
            self.ffn_pre = None
            self.pass_F(L, 1, preloaded=pre)
        self.pass_O()
        self.p.emit()
        return self.nc


def host_consts():
    ident = np.eye(128, dtype=np.float32)
    i = np.arange(128)
    tri = np.zeros((4, 128, 128), np.float32)
    tri[0] = (i[:, None] <= i[None, :])
    tri[1] = (i[:, None] >= i[None, :])
    tri[2] = (i[:, None] > i[None, :])
    tri[3] = (i[:, None] < i[None, :])
    half = np.arange(32, dtype=np.float32)
    inv = (1.0 / (10000.0 ** (np.arange(0, 64, 2, dtype=np.float32) / 64.0))).astype(np.float32)
    invf = np.concatenate([inv, inv]).reshape(64, 1).astype(np.float32)
    return ident, tri, invf


def core_inputs(xsegs, csegs, link, pos_off1, weights):
    SEG = xsegs[0].shape[0]
    ident, tri, invf = host_consts()
    x = np.ascontiguousarray(np.concatenate(xsegs, axis=0))
    c = np.stack(csegs, axis=0)
    crows = np.ascontiguousarray(c.reshape(3, 8, 128).transpose(1, 0, 2).reshape(24, 128))
    flags = np.zeros((128, 4), np.float32)
    flags[:, 0] = link
    flags[:, 1] = 0.0 if link else -30000.0
    ar = np.arange(SEG, dtype=np.float32)
    pos = np.concatenate([ar, ar + pos_off1, ar]).reshape(1, 3 * SEG).astype(np.float32)
    m = {"x": x, "crows": crows, "flags": flags, "pos": pos, "ident": ident, "tri": tri, "invf": invf}
    m.update(weights)
    return m


WNAMES = ['ada_w', 'ada_b', 'norm_g', 'ffn_w_gate', 'ffn_w_up', 'ffn_w_down', 'gla_w_in', 'gla_w_gate_up',
          'gla_b_gate', 'gla_g_norm', 'gla_w_out', 'mla_w_in', 'mla_g_q', 'mla_g_kv', 'mla_w_uq', 'mla_w_ukv',
          'mla_w_out', 'final_ada_w', 'final_ada_b', 'final_g']


def kernel(**inputs):
    inp = {k: np.asarray(v) for k, v in inputs.items()}
    weights = {k: np.ascontiguousarray(inp[k], dtype=np.float32) for k in WNAMES}
    xp, xs_, cp, cs = inp['x_prompt'], inp['x_sample'], inp['c_prompt'], inp['c_sample']
    SEG = xp.shape[1]
    in_maps = []
    for core in range(8):
        if core < 4:
            xsegs = [xs_[core, 0:SEG], xs_[core, SEG:2 * SEG], xp[core]]
            csegs = [cs[core], cs[core], cp[core]]
            in_maps.append(core_inputs(xsegs, csegs, 1.0, float(SEG), weights))
        else:
            b = 4 + 3 * (core - 4)
            xsegs = [xp[b], xp[b + 1], xp[b + 2]]
            csegs = [cp[b], cp[b + 1], cp[b + 2]]
            in_maps.append(core_inputs(xsegs, csegs, 0.0, 0.0, weights))
    nc = K({'SEG': SEG}).build()
    res = run_bass_kernel_spmd(nc, in_maps, core_ids=list(range(8)))
    y_prompt = np.zeros(xp.shape, np.float32)
    y_sample = np.zeros(xs_.shape, np.float32)
    for core in range(8):
        y = np.asarray(res.results[core]["y"], dtype=np.float32)
        if core < 4:
            y_sample[core, 0:SEG] = y[0:SEG]
            y_sample[core, SEG:2 * SEG] = y[SEG:2 * SEG]
            y_prompt[core] = y[2 * SEG:]
        else:
            b = 4 + 3 * (core - 4)
            for i in range(3):
                y_prompt[b + i] = y[i * SEG:(i + 1) * SEG]
    return (y_prompt, y_sample)
```
